# Optimizing a Trainium2 kernel written in Bass

```python
import jax, jax.numpy as jnp
from jax import lax
import numpy as np

D_MODEL = 4096
BATCH = 4
SEQ = 4096
DEPTH = 2

CHUNK = 64
N_MIXERS = 2
N_RET_LAYERS = (DEPTH + N_MIXERS - 1) // N_MIXERS
N_LRU_LAYERS = DEPTH // N_MIXERS

RET_HEADS = 16
RET_QK_DIM = D_MODEL // RET_HEADS
RET_V_DIM = 2 * RET_QK_DIM
RET_V_WIDTH = RET_HEADS * RET_V_DIM
ROPE_THETA = 10000.0

LRU_WIDTH = ((4 * D_MODEL // 3) // 128) * 128
LRU_BLOCKS = 16
LRU_BLOCK = LRU_WIDTH // LRU_BLOCKS
CONV_WIDTH = 4
LRU_C = 8.0

D_FF = 4 * D_MODEL
EPS = 1e-6

kernel_name = "retention_rglru_interleaved_trunk"


def rms_norm(x, g):
    xf = x.astype(jnp.float32)
    xf = xf * lax.rsqrt(jnp.mean(xf * xf, axis=-1, keepdims=True) + EPS)
    return (xf * g.astype(jnp.float32)).astype(x.dtype)


def rope(t, positions):
    half = t.shape[-1] // 2
    inv_freq = ROPE_THETA ** (-jnp.arange(half, dtype=jnp.float32) / half)
    ang = positions.astype(jnp.float32)[:, :, None, None] * inv_freq
    cos = jnp.cos(ang).astype(t.dtype)
    sin = jnp.sin(ang).astype(t.dtype)
    t1, t2 = t[..., :half], t[..., half:]
    return jnp.concatenate([t1 * cos - t2 * sin, t2 * cos + t1 * sin], axis=-1)


def retention_mixer(h, positions, w_in, w_out):
    B, S, _ = h.shape
    nc = S // CHUNK
    proj = h @ w_in
    q, k, v, g = jnp.split(proj, [D_MODEL, 2 * D_MODEL, 2 * D_MODEL + RET_V_WIDTH], axis=-1)
    q = rope(q.reshape(B, S, RET_HEADS, RET_QK_DIM), positions)
    k = rope(k.reshape(B, S, RET_HEADS, RET_QK_DIM), positions) * (RET_QK_DIM ** -0.5)
    v = v.reshape(B, S, RET_HEADS, RET_V_DIM)

    def to_chunks(t):
        return t.reshape(B, nc, CHUNK, RET_HEADS, -1).transpose(1, 0, 3, 2, 4)

    log_g = jnp.log1p(-jnp.exp2(-5.0 - jnp.arange(RET_HEADS, dtype=jnp.float32)))
    idx = jnp.arange(CHUNK, dtype=jnp.float32)
    dt = q.dtype
    intra = jnp.exp(log_g[:, None, None] * jnp.abs(idx[:, None] - idx[None, :])).astype(dt)
    q_dec = jnp.exp(log_g[:, None] * (idx + 1.0)).astype(dt)[None, :, :, None]
    k_dec = jnp.exp(log_g[:, None] * (CHUNK - 1.0 - idx)).astype(dt)[None, :, :, None]
    chunk_dec = jnp.exp(log_g * CHUNK).astype(dt)[None, :, None, None]

    def step(state, qkv):
        qc, kc, vc = qkv
        scores = jnp.einsum('bhqd,bhkd->bhqk', qc, kc) * intra[None]
        o = (jnp.einsum('bhqk,bhkv->bhqv', scores, vc)
             + jnp.einsum('bhqd,bhdv->bhqv', qc, state) * q_dec)
        state = state * chunk_dec + jnp.einsum('bhkd,bhkv->bhdv', kc * k_dec, vc)
        return state, o

    state0 = jnp.zeros((B, RET_HEADS, RET_QK_DIM, RET_V_DIM), v.dtype)
    _, o = lax.scan(step, state0, (to_chunks(q), to_chunks(k), to_chunks(v)))
    o = o.transpose(1, 0, 3, 2, 4).reshape(B, S, RET_HEADS, RET_V_DIM)
    of = o.astype(jnp.float32)
    of = of * lax.rsqrt(jnp.mean(of * of, axis=-1, keepdims=True) + EPS)
    o = of.astype(h.dtype).reshape(B, S, RET_V_WIDTH) * jax.nn.silu(g)
    return o @ w_out


def rglru_mixer(h, w_in, conv_w, conv_b, w_rgate, b_rgate, w_igate, b_igate, lam, w_out):
    B, S, _ = h.shape
    proj = h @ w_in
    xb, gb = jnp.split(proj, [LRU_WIDTH], axis=-1)
    xc = lax.conv_general_dilated(
        xb, conv_w.reshape(CONV_WIDTH, 1, LRU_WIDTH).astype(xb.dtype),
        window_strides=(1,), padding=[(CONV_WIDTH - 1, 0)],
        dimension_numbers=('NWC', 'WIO', 'NWC'),
        feature_group_count=LRU_WIDTH) + conv_b
    xblk = xc.reshape(B, S, LRU_BLOCKS, LRU_BLOCK)
    r = jax.nn.sigmoid(jnp.einsum('bsnc,ncd->bsnd', xblk, w_rgate).reshape(B, S, LRU_WIDTH)
                       .astype(jnp.float32) + b_rgate.astype(jnp.float32))
    i = jax.nn.sigmoid(jnp.einsum('bsnc,ncd->bsnd', xblk, w_igate).reshape(B, S, LRU_WIDTH)
                       .astype(jnp.float32) + b_igate.astype(jnp.float32))
    log_a = -LRU_C * r * jax.nn.softplus(-lam.astype(jnp.float32))
    a = jnp.exp(log_a)
    u = jnp.sqrt(-jnp.expm1(2.0 * log_a)) * (i * xc.astype(jnp.float32))

    def step(hc, au):
        a_t, u_t = au
        hc = a_t * hc + u_t
        return hc, hc

    h0 = jnp.zeros((B, LRU_WIDTH), jnp.float32)
    _, hs = lax.scan(step, h0, (a.transpose(1, 0, 2), u.transpose(1, 0, 2)))
    y = hs.transpose(1, 0, 2).astype(h.dtype) * jax.nn.gelu(gb)
    return y @ w_out


def squared_relu_mlp(h, w_up, w_down):
    z = jax.nn.relu(h @ w_up)
    return (z * z) @ w_down


def setup_inputs(seed: int = 0) -> dict:
    key = jax.random.key(seed)
    ks = jax.random.split(key, 20)
    f32 = jnp.float32
    nrm = lambda k, shape, scale: jax.random.normal(k, shape, f32) * scale
    x = jax.random.normal(ks[0], (BATCH, SEQ, D_MODEL), f32)
    positions = jnp.broadcast_to(jnp.arange(SEQ, dtype=jnp.int32)[None, :], (BATCH, SEQ)).astype(jnp.int32)
    norm_mix_g = 1.0 + nrm(ks[1], (DEPTH, D_MODEL), 0.02)
    norm_mlp_g = 1.0 + nrm(ks[2], (DEPTH, D_MODEL), 0.02)
    final_norm_g = 1.0 + nrm(ks[3], (D_MODEL,), 0.02)
    ret_in_width = 2 * D_MODEL + 2 * RET_V_WIDTH
    ret_w_in = nrm(ks[4], (N_RET_LAYERS, D_MODEL, ret_in_width), D_MODEL ** -0.5)
    ret_w_out = nrm(ks[5], (N_RET_LAYERS, RET_V_WIDTH, D_MODEL), RET_V_WIDTH ** -0.5)
    lru_w_in = nrm(ks[6], (N_LRU_LAYERS, D_MODEL, 2 * LRU_WIDTH), D_MODEL ** -0.5)
    lru_conv_w = nrm(ks[7], (N_LRU_LAYERS, CONV_WIDTH, LRU_WIDTH), CONV_WIDTH ** -0.5)
    lru_conv_b = nrm(ks[8], (N_LRU_LAYERS, LRU_WIDTH), 0.01)
    lru_w_rgate = nrm(ks[9], (N_LRU_LAYERS, LRU_BLOCKS, LRU_BLOCK, LRU_BLOCK), LRU_BLOCK ** -0.5)
    lru_b_rgate = nrm(ks[10], (N_LRU_LAYERS, LRU_WIDTH), 0.01)
    lru_w_igate = nrm(ks[11], (N_LRU_LAYERS, LRU_BLOCKS, LRU_BLOCK, LRU_BLOCK), LRU_BLOCK ** -0.5)
    lru_b_igate = nrm(ks[12], (N_LRU_LAYERS, LRU_WIDTH), 0.01)
    a_c = jax.random.uniform(ks[13], (N_LRU_LAYERS, LRU_WIDTH), f32, 0.9, 0.999)
    s = a_c ** (1.0 / LRU_C)
    lru_lambda = jnp.log(s) - jnp.log1p(-s)
    lru_w_out = nrm(ks[14], (N_LRU_LAYERS, LRU_WIDTH, D_MODEL), LRU_WIDTH ** -0.5)
    mlp_w_up = nrm(ks[15], (DEPTH, D_MODEL, D_FF), D_MODEL ** -0.5)
    mlp_w_down = nrm(ks[16], (DEPTH, D_FF, D_MODEL), D_FF ** -0.5)
    return {"x": x, "positions": positions, "norm_mix_g": norm_mix_g, "norm_mlp_g": norm_mlp_g,
            "final_norm_g": final_norm_g, "ret_w_in": ret_w_in, "ret_w_out": ret_w_out,
            "lru_w_in": lru_w_in, "lru_conv_w": lru_conv_w, "lru_conv_b": lru_conv_b,
            "lru_w_rgate": lru_w_rgate, "lru_b_rgate": lru_b_rgate, "lru_w_igate": lru_w_igate,
            "lru_b_igate": lru_b_igate, "lru_lambda": lru_lambda, "lru_w_out": lru_w_out,
            "mlp_w_up": mlp_w_up, "mlp_w_down": mlp_w_down}


def reference(x, positions, norm_mix_g, norm_mlp_g, final_norm_g, ret_w_in, ret_w_out,
              lru_w_in, lru_conv_w, lru_conv_b, lru_w_rgate, lru_b_rgate, lru_w_igate,
              lru_b_igate, lru_lambda, lru_w_out, mlp_w_up, mlp_w_down):
    h = x
    for i in range(DEPTH):
        j = i // N_MIXERS
        y = rms_norm(h, norm_mix_g[i])
        if i % N_MIXERS == 0:
            h = h + retention_mixer(y, positions, ret_w_in[j], ret_w_out[j])
        else:
            h = h + rglru_mixer(y, lru_w_in[j], lru_conv_w[j], lru_conv_b[j], lru_w_rgate[j],
                                lru_b_rgate[j], lru_w_igate[j], lru_b_igate[j], lru_lambda[j],
                                lru_w_out[j])
        y = rms_norm(h, norm_mlp_g[i])
        h = h + squared_relu_mlp(y, mlp_w_up[i], mlp_w_down[i])
    return rms_norm(h, final_norm_g)
```

```python
import os
import numpy as np
from contextlib import ExitStack
import concourse.bass as bass
import concourse.mybir as mybir
from concourse.bass_utils import run_bass_kernel_spmd

F32 = mybir.dt.float32
BF16 = mybir.dt.bfloat16
I32 = mybir.dt.int32
AF = mybir.ActivationFunctionType
ALU = mybir.AluOpType

EPS = 1e-6
LRU_C = 8.0
ROPE_THETA = 10000.0


class Cfg:
    def __init__(self, D=4096, SEQ=4096, B=4, H=16, LB=16, FF=16384, SEG=1024, NCORES=4):
        self.D, self.SEQ, self.B, self.H, self.LB, self.FF, self.SEG, self.NCORES = D, SEQ, B, H, LB, FF, SEG, NCORES
        self.DK, self.DV = 256, 512
        self.HV = H * self.DV
        self.LBS = 336
        self.LW = LB * self.LBS
        self.LC = self.LW // 112
        self.KC = D // 128
        self.FC = FF // 128
        self.TN = min(512, SEG)
        self.NT = SEG // self.TN
        self.NJ = SEG // 128
        self.SPC = B // NCORES
        self.SPS = SEQ // SEG
        self.NSEG = self.SPC * self.SPS
        assert D == H * self.DK and SEG % 128 == 0 and SEQ % SEG == 0 and B % NCORES == 0


ENGS = ("pe", "act", "dve", "pool", "sp")
BLOCK_ATTR = {"pe": "tensor", "act": "scalar", "dve": "vector", "pool": "gpsimd", "sp": "sync"}
NDSEM = 8


class Op:
    __slots__ = ("eng", "fn", "deps", "sig", "is_dma", "tok", "pre")

    def __init__(self, eng, fn, is_dma):
        self.eng, self.fn, self.is_dma = eng, fn, is_dma
        self.deps = ()
        self.sig = is_dma
        self.tok = None
        self.pre = None


class Blk:
    def __init__(self, nc, name):
        self.nc, self.name = nc, name
        self.ops = {e: [] for e in ENGS}
        self.last_w = {}
        self.readers = {}

    def add(self, eng, fn, reads=(), writes=(), dma=False):
        op = Op(eng, fn, dma)
        deps = set()
        for k in reads:
            w = self.last_w.get(k)
            if w is not None:
                deps.add(w)
        for k in writes:
            w = self.last_w.get(k)
            if w is not None:
                deps.add(w)
            for r in self.readers.get(k, ()):
                deps.add(r)
        if eng == "pe":
            deps = {d for d in deps if d.eng != "pe"}
        for d in deps:
            d.sig = True
        op.deps = tuple(deps)
        for k in reads:
            lst = self.readers.setdefault(k, [])
            if not dma:
                lst[:] = [r for r in lst if r.is_dma or r.eng != eng]
            lst.append(op)
        for k in writes:
            self.last_w[k] = op
            self.readers[k] = []
        self.ops[eng].append(op)
        return op

    def dma(self, q, out, in_, reads=(), writes=(), **kw):
        return self.add(q, lambda e: e.dma_start(out=out, in_=in_, **kw), reads, writes, dma=True)

    def emit(self):
        nc = self.nc
        sems = {}
        for e in ENGS:
            if e != "sp" and any(not o.is_dma for o in self.ops[e]):
                sems[("c", e)] = nc.alloc_semaphore(name=f"{self.name}_c{e}")
            if any(o.is_dma for o in self.ops[e]):
                for i in range(NDSEM):
                    sems[("d", e, i)] = nc.alloc_semaphore(name=f"{self.name}_d{e}{i}")
        finals = {e: {} for e in ENGS}
        for e in ENGS:
            cnt = 0
            k = 0
            dcount = [0] * NDSEM
            for op in self.ops[e]:
                if op.is_dma:
                    i = k % NDSEM
                    k += 1
                    if dcount[i]:
                        op.pre = (("d", e, i), dcount[i])
                    dcount[i] += 16
                    op.tok = (("d", e, i), dcount[i])
                    finals[e][("d", e, i)] = dcount[i]
                elif op.sig:
                    cnt += 1
                    op.tok = (("c", e), cnt)
        with nc.Block() as block:
            for e in ENGS:
                if not self.ops[e]:
                    continue

                def body(eng, e=e):
                    known = {}
                    for op in self.ops[e]:
                        waits = {}
                        for d in op.deps:
                            s, v = d.tok
                            if waits.get(s, 0) < v:
                                waits[s] = v
                        if op.pre is not None:
                            s, v = op.pre
                            if waits.get(s, 0) < v:
                                waits[s] = v
                        for s, v in waits.items():
                            if known.get(s, 0) < v:
                                eng.wait_ge(sems[s], v)
                                known[s] = v
                        ins = op.fn(eng)
                        if op.tok is not None:
                            ins.then_inc(sems[op.tok[0]], 16 if op.is_dma else 1)
                    for s, v in finals[e].items():
                        if known.get(s, 0) < v:
                            eng.wait_ge(sems[s], v)

                getattr(block, BLOCK_ATTR[e])(body)
        nc.clear_and_free_semaphores(list(sems.values()))
        nc.all_engine_barrier()


class Rot:
    def __init__(self, n):
        self.n, self.i = n, 0

    def next(self):
        r = self.i % self.n
        self.i += 1
        return r


def _phase(fn):
    def wrapped(self, *a, **k):
        self._ph += 1
        if self._ph > self._maxph:
            return
        return fn(self, *a, **k)
    return wrapped


class Builder:
    def __init__(self, cfg, debug=False):
        self.c = c = cfg
        self.debug = debug
        self.nc = nc = bass.Bass("TRN2", target_bir_lowering=False)
        ein = lambda name, shape, dt=F32: nc.dram_tensor(name, list(shape), dt, kind="ExternalInput").ap()
        KC, FC, LC, H = c.KC, c.FC, c.LC, c.H
        self.xT = ein("xT", [c.NSEG, c.D, c.SEG])
        self.pos = ein("pos", [c.NSEG, c.SEG], I32)
        self.gains = ein("gains", [128, 5, KC])
        self.w_in = ein("w_in", [H * 12, 128, KC, 128])
        self.w_ro = ein("w_ro", [KC, 128, H * 4, 128])
        self.w_li = ein("w_li", [2 * LC, 128, KC, 112])
        self.w_lo = ein("w_lo", [KC, 112, LC, 128])
        self.w_up = ein("w_up", [2, FC, 128, KC, 128])
        self.w_dn = ein("w_dn", [2, KC, 128, FC, 128])
        self.w_gt = ein("w_gt", [c.LB, 112, 2, 3, 336])
        self.lru_p = ein("lru_p", [112, LC, 8])
        self.cmask = ein("cmask", [128, H, 128])
        self.ccol = ein("ccol", [128, 8 + 3 * H])
        self.ident = ein("ident", [128, 128])
        self.outT = nc.dram_tensor("outT", [c.NSEG, c.D, c.SEG], F32, kind="ExternalOutput").ap()
        kind = "ExternalOutput" if debug else "Internal"
        scr = lambda name, shape, dt=F32: nc.dram_tensor(name, list(shape), dt, kind=kind).ap()
        self.hA = scr("hA", [c.D, c.SEG])
        self.hB = scr("hB", [c.D, c.SEG])
        self.hC = scr("hC", [c.D, c.SEG])
        self.hD = scr("hD", [c.D, c.SEG])
        self.og = scr("og", [H * 4, 128, c.SEG], BF16)
        self.z = scr("z", [FC, 128, c.SEG], BF16)
        self.yg = scr("yg", [LC, 112, c.SEG], BF16)
        self.rst = scr("rst", [H, 128, 2, 512])
        self.dbg_y = scr("dbg_y", [c.D, c.SEG], BF16)

    def gemm(self, blk, es, w_chunks, kp, KCn, M, x_fm, xkey, nslots, banks, ps, epilogue, tn_list, wname="w"):
        nc = self.nc
        wsl = es.enter_context(nc.sbuf_tensor(f"{blk.name}_{wname}sl", [128, nslots, KCn, M], BF16))
        rot = Rot(nslots)
        for n, wd in enumerate(w_chunks):
            s = rot.next()
            blk.dma("pool", wsl[0:kp, s, :, :], wd, writes=[(wname, s)], max_dma_last_dim=8192)
            for (t0, tw) in tn_list:
                b = banks.next()
                for kc in range(KCn):
                    blk.add("pe", lambda e, s=s, kc=kc, b=b, t0=t0, tw=tw: e.matmul(
                        ps[0:M, b, 0:tw], wsl[0:kp, s, kc, :], x_fm[0:kp, kc, t0:t0 + tw],
                        start=(kc == 0), stop=(kc == KCn - 1)),
                        reads=[(wname, s), xkey], writes=[("ps", b)])
                epilogue(n, t0, tw, b)

    def consts(self, blk, es, want=("gains",)):
        nc, c = self.nc, self.c
        out = {}
        if "gains" in want:
            g = es.enter_context(nc.sbuf_tensor(f"{blk.name}_gains", [128, 5, c.KC], F32))
            blk.dma("sp", g[:], self.gains, writes=["gains"])
            out["gains"] = g
        if "ccol" in want:
            t = es.enter_context(nc.sbuf_tensor(f"{blk.name}_ccol", [128, 8 + 3 * c.H], F32))
            blk.dma("sp", t[:], self.ccol, writes=["ccol"])
            out["ccol"] = t
        if "ident" in want:
            t = es.enter_context(nc.sbuf_tensor(f"{blk.name}_ident", [128, 128], BF16))
            blk.dma("pool", t[:], self.ident, writes=["ident"])
            out["ident"] = t
        if "cmask" in want:
            t = es.enter_context(nc.sbuf_tensor(f"{blk.name}_cmask", [128, c.H, 128], F32))
            blk.dma("sp", t[:], self.cmask, writes=["cmask"])
            out["cmask"] = t
        return out

    @_phase
    def norm(self, name, h_src, gidx, y_fm=None, out_dram=None):
        nc, c = self.nc, self.c
        KC, TN = c.KC, c.TN
        blk = Blk(nc, name)
        with ExitStack() as es:
            ht = es.enter_context(nc.sbuf_tensor(f"{name}_ht", [128, 2, KC, TN], F32))
            sq = es.enter_context(nc.sbuf_tensor(f"{name}_sq", [128, 2, TN], BF16))
            rstd = es.enter_context(nc.sbuf_tensor(f"{name}_rstd", [128, 2, TN], F32))
            ones = es.enter_context(nc.sbuf_tensor(f"{name}_ones", [128, 128], BF16))
            ps = es.enter_context(nc.psum_tensor(f"{name}_ps", [128, 2, 512], F32))
            ost = None
            if out_dram is not None:
                ost = es.enter_context(nc.sbuf_tensor(f"{name}_ost", [128, 2, TN], F32))
            cs = self.consts(blk, es, ("gains",))
            gains = cs["gains"]
            blk.add("pool", lambda e: e.memset(ones[:], 1.0), writes=["ones"])
            hv = h_src.rearrange("(kc p) t -> p kc t", p=128)
            G = 4 if KC >= 4 else 1
            per = KC // G
            for tt in range(c.NT):
                t0 = tt * TN
                hb = tt % 2
                for g in range(G):
                    blk.dma("sp", ht[:, hb, g * per:(g + 1) * per, :], hv[:, g * per:(g + 1) * per, t0:t0 + TN],
                            writes=[("ht", hb, g)])
                b = tt % 2
                for kc in range(KC):
                    s = kc % 2
                    blk.add("act", lambda e, kc=kc, s=s, hb=hb: e.activation(sq[:, s, :], ht[:, hb, kc, :], AF.Square),
                            reads=[("ht", hb, kc // per)], writes=[("sq", s)])
                    blk.add("pe", lambda e, kc=kc, s=s, b=b: e.matmul(ps[:, b, 0:TN], ones[:], sq[:, s, :],
                                                                     start=(kc == 0), stop=(kc == KC - 1)),
                            reads=["ones", ("sq", s)], writes=[("ps", b)])
                blk.add("dve", lambda e, b=b, hb=hb: e.tensor_scalar(rstd[:, hb, :], ps[:, b, 0:TN], 1.0 / c.D, EPS,
                                                                   op0=ALU.mult, op1=ALU.add),
                        reads=[("ps", b)], writes=[("rstd", hb)])
                blk.add("act", lambda e, hb=hb: e.activation(rstd[:, hb, :], rstd[:, hb, :], AF.Sqrt),
                        reads=[("rstd", hb)], writes=[("rstd", hb)])
                blk.add("dve", lambda e, hb=hb: e.reciprocal(rstd[:, hb, :], rstd[:, hb, :]),
                        reads=[("rstd", hb)], writes=[("rstd", hb)])
                for kc in range(KC):
                    if y_fm is not None:
                        blk.add("dve", lambda e, kc=kc, t0=t0, hb=hb: e.scalar_tensor_tensor(
                            y_fm[:, kc, t0:t0 + TN], ht[:, hb, kc, :], gains[:, gidx, kc:kc + 1], rstd[:, hb, :],
                            op0=ALU.mult, op1=ALU.mult),
                            reads=[("ht", hb, kc // per), ("rstd", hb), "gains"], writes=["y"])
                    else:
                        s = kc % 2
                        blk.add("dve", lambda e, kc=kc, s=s, hb=hb: e.scalar_tensor_tensor(
                            ost[:, s, :], ht[:, hb, kc, :], gains[:, gidx, kc:kc + 1], rstd[:, hb, :],
                            op0=ALU.mult, op1=ALU.mult),
                            reads=[("ht", hb, kc // per), ("rstd", hb), "gains"], writes=[("ost", s)])
                        blk.dma("sp", out_dram[kc * 128:(kc + 1) * 128, t0:t0 + TN], ost[:, s, :],
                                reads=[("ost", s)])
            if self.debug and y_fm is not None:
                blk.dma("sp", self.dbg_y.rearrange("(kc p) t -> p kc t", p=128), y_fm[:], reads=["y"])
            blk.emit()

    def resid_epilogue(self, blk, es, ps, h_src, h_dst, nbuf=2):
        nc, c = self.nc, self.c
        TN = c.TN
        hin = es.enter_context(nc.sbuf_tensor(f"{blk.name}_hin", [128, nbuf, TN], F32))
        hout = es.enter_context(nc.sbuf_tensor(f"{blk.name}_hout", [128, nbuf, TN], F32))
        rot = Rot(nbuf)

        def ep(n, t0, tw, b):
            s = rot.next()
            blk.dma("sp", hin[:, s, 0:tw], h_src[n * 128:(n + 1) * 128, t0:t0 + tw], writes=[("hin", s)])
            blk.add("dve", lambda e: e.tensor_tensor(hout[:, s, 0:tw], ps[:, b, 0:tw], hin[:, s, 0:tw], ALU.add),
                    reads=[("ps", b), ("hin", s)], writes=[("hout", s)])
            blk.dma("sp", h_dst[n * 128:(n + 1) * 128, t0:t0 + tw], hout[:, s, 0:tw], reads=[("hout", s)])
        return ep

    @_phase
    def ret_a(self, name, y_fm, seg, first):
        nc, c = self.nc, self.c
        KC, TN, SEG, NJ, H = c.KC, c.TN, c.SEG, c.NJ, c.H
        blk = Blk(nc, name)
        with ExitStack() as es:
            sb = lambda nm, shape, dt=F32: es.enter_context(nc.sbuf_tensor(f"{name}_{nm}", shape, dt))
            cs = self.consts(blk, es, ("ccol", "ident", "cmask"))
            ccol, ident, cmask = cs["ccol"], cs["ident"], cs["cmask"]
            posi = sb("posi", [128, SEG], I32)
            ang = sb("ang", [128, SEG])
            tmp = sb("tmp", [128, SEG])
            tmp2 = sb("tmp2", [128, SEG])
            tabs = sb("tabs", [128, 4, SEG])
            blk.dma("sp", posi[:], self.pos[seg].partition_broadcast(128), writes=["posi"])
            blk.add("dve", lambda e: e.tensor_copy(ang[:], posi[:]), reads=["posi"], writes=["ang"])
            blk.add("dve", lambda e: e.tensor_scalar(ang[:], ang[:], ccol[:, 0:1], None, op0=ALU.mult),
                    reads=["ang", "ccol"], writes=["ang"])
            blk.add("dve", lambda e: e.tensor_scalar(ang[:], ang[:], float(1.0 / (2 * np.pi)), None, op0=ALU.mult),
                    reads=["ang"], writes=["ang"])
            for ti, off in ((1, 0.0), (0, 0.25)):
                blk.add("dve", lambda e, off=off: e.tensor_scalar(tmp[:], ang[:], off, None, op0=ALU.add),
                        reads=["ang"], writes=["tmp"])
                blk.add("dve", lambda e: e.tensor_copy(posi[:], tmp[:]), reads=["tmp"], writes=["posi"])
                blk.add("dve", lambda e: e.tensor_copy(tmp2[:], posi[:]), reads=["posi"], writes=["tmp2"])
                blk.add("dve", lambda e: e.tensor_tensor(tmp[:], tmp[:], tmp2[:], ALU.subtract),
                        reads=["tmp", "tmp2"], writes=["tmp"])
                blk.add("dve", lambda e: e.tensor_scalar(tmp2[:], tmp[:], 0.0, None, op0=ALU.is_lt),
                        reads=["tmp"], writes=["tmp2"])
                blk.add("dve", lambda e: e.tensor_tensor(tmp[:], tmp[:], tmp2[:], ALU.add),
                        reads=["tmp", "tmp2"], writes=["tmp"])
                blk.add("act", lambda e, ti=ti: e.activation(tabs[:, ti, :], tmp[:], AF.Sin, bias=ccol[:, 1:2],
                                                             scale=float(-2 * np.pi)),
                        reads=["tmp", "ccol"], writes=["tabs"])
            blk.add("act", lambda e: e.mul(tabs[:, 2:4, :], tabs[:, 0:2, :], float(c.DK ** -0.5)),
                    reads=["tabs"], writes=["tabs"])

            NSL = 4
            wsl = sb("wsl", [128, NSL, KC, 128], BF16)
            wrot = Rot(NSL)
            ps = es.enter_context(nc.psum_tensor(f"{name}_ps", [128, 2, 512], F32))
            psT = es.enter_context(nc.psum_tensor(f"{name}_psT", [128, 2, 8, 128], BF16))
            psS = es.enter_context(nc.psum_tensor(f"{name}_psS", [128, 512], F32))
            psO = es.enter_context(nc.psum_tensor(f"{name}_psO", [128, 512], F32))
            psU = es.enter_context(nc.psum_tensor(f"{name}_psU", [128, 2, 512], F32))
            banks = Rot(2)
            trot = Rot(2)
            qf = sb("qf", [128, 2, SEG], BF16)
            kf = sb("kf", [128, 2, SEG], BF16)
            ktm = sb("ktm", [128, NJ, 256], BF16)
            vf = sb("vf", [128, 4, SEG], BF16)
            vtm = sb("vtm", [128, NJ, 512], BF16)
            sg = sb("sg", [128, 4, SEG], BF16)
            rt = sb("rt", [128, 4, TN])
            sT = sb("sT", [128, 2, 128], BF16)
            junk = sb("junk", [128, 512], BF16)
            onb = sb("onb", [128, 2, 512], BF16)
            ogs = sb("ogs", [128, 4, SEG], BF16)
            stf = sb("stf", [128, 2, 512])
            stb = sb("stb", [128, 2, 2, 512], BF16)
            ssp = sb("ssp", [128, 2, 4])

            def load_w(n):
                s = wrot.next()
                blk.dma("pool", wsl[:, s, :, :], self.w_in[n], writes=[("w", s)], max_dma_last_dim=8192)
                return s

            def mm_chunk(s, t0, tw):
                b = banks.next()
                for kc in range(KC):
                    blk.add("pe", lambda e, kc=kc: e.matmul(ps[:, b, 0:tw], wsl[:, s, kc, :], y_fm[:, kc, t0:t0 + tw],
                                                           start=(kc == 0), stop=(kc == KC - 1)),
                            reads=[("w", s)], writes=[("ps", b)])
                return b

            for h in range(H):
                for which, dst, ci, si in (("q", qf, 0, 1), ("k", kf, 2, 3)):
                    base = h * 12 + (0 if which == "q" else 2)
                    s0 = load_w(base)
                    s1 = load_w(base + 1)
                    for tt in range(c.NT):
                        t0 = tt * TN
                        bA = mm_chunk(s0, t0, TN)
                        bB = mm_chunk(s1, t0, TN)
                        A, Bp = ps[:, bA, 0:TN], ps[:, bB, 0:TN]
                        cos, sin = tabs[:, ci, t0:t0 + TN], tabs[:, si, t0:t0 + TN]
                        blk.add("dve", lambda e, A=A, cos=cos: e.tensor_tensor(rt[:, 0, :], A, cos, ALU.mult),
                                reads=[("ps", bA), "tabs"], writes=[("rt", 0)])
                        blk.add("dve", lambda e, Bp=Bp, sin=sin: e.tensor_tensor(rt[:, 1, :], Bp, sin, ALU.mult),
                                reads=[("ps", bB), "tabs"], writes=[("rt", 1)])
                        blk.add("dve", lambda e, dst=dst, t0=t0: e.tensor_tensor(dst[:, 0, t0:t0 + TN], rt[:, 0, :], rt[:, 1, :], ALU.subtract),
                                reads=[("rt", 0), ("rt", 1)], writes=[which])
                        blk.add("dve", lambda e, Bp=Bp, cos=cos: e.tensor_tensor(rt[:, 2, :], Bp, cos, ALU.mult),
                                reads=[("ps", bB), "tabs"], writes=[("rt", 2)])
                        blk.add("dve", lambda e, A=A, sin=sin: e.tensor_tensor(rt[:, 3, :], A, sin, ALU.mult),
                                reads=[("ps", bA), "tabs"], writes=[("rt", 3)])
                        blk.add("dve", lambda e, dst=dst, t0=t0: e.tensor_tensor(dst[:, 1, t0:t0 + TN], rt[:, 2, :], rt[:, 3, :], ALU.add),
                                reads=[("rt", 2), ("rt", 3)], writes=[which])
                for j in range(NJ):
                    tr = trot.next()
                    for dc in range(2):
                        blk.add("pe", lambda e, j=j, dc=dc, tr=tr: e.transpose(psT[:, tr, dc, :], kf[:, dc, j * 128:(j + 1) * 128], ident[:]),
                                reads=["k", "ident"], writes=[("T", tr)])
                    blk.add("dve", lambda e, j=j, tr=tr, h=h: e.tensor_scalar(
                        ktm[:, j, :].rearrange("p (a b) -> p a b", a=2), psT[:, tr, 0:2, :], ccol[:, 8 + H + h:9 + H + h], None, op0=ALU.mult),
                        reads=[("T", tr), "ccol"], writes=["ktm"])
                for vc in range(4):
                    s = load_w(h * 12 + 4 + vc)
                    for tt in range(c.NT):
                        t0 = tt * TN
                        b = mm_chunk(s, t0, TN)
                        blk.add("act", lambda e, vc=vc, t0=t0, b=b: e.copy(vf[:, vc, t0:t0 + TN], ps[:, b, 0:TN]),
                                reads=[("ps", b)], writes=["vf"])
                for j in range(NJ):
                    tr = trot.next()
                    for vc in range(4):
                        blk.add("pe", lambda e, j=j, vc=vc, tr=tr: e.transpose(psT[:, tr, vc, :], vf[:, vc, j * 128:(j + 1) * 128], ident[:]),
                                reads=["vf", "ident"], writes=[("T", tr)])
                    blk.add("act", lambda e, j=j, tr=tr: e.copy(vtm[:, j, :].rearrange("p (a b) -> p a b", a=4), psT[:, tr, 0:4, :]),
                            reads=[("T", tr)], writes=["vtm"])
                def g_gen(h=h):
                    for vc in range(4):
                        s = load_w(h * 12 + 8 + vc)
                        for tt in range(c.NT):
                            t0 = tt * TN
                            b = banks.next()
                            for kc in range(KC):
                                blk.add("pe", lambda e, kc=kc, b=b, s=s, t0=t0: e.matmul(
                                    ps[:, b, 0:TN], wsl[:, s, kc, :], y_fm[:, kc, t0:t0 + TN],
                                    start=(kc == 0), stop=(kc == KC - 1)),
                                    reads=[("w", s)], writes=[("ps", b)])
                                yield
                            blk.add("act", lambda e, vc=vc, t0=t0, b=b: e.activation(sg[:, vc, t0:t0 + TN], ps[:, b, 0:TN], AF.Silu),
                                    reads=[("ps", b)], writes=["sg"])
                gen = g_gen()

                def pump(n):
                    for _ in range(n):
                        try:
                            next(gen)
                        except StopIteration:
                            return
                if first:
                    blk.add("pool", lambda e: e.memset(stf[:], 0.0), writes=["stf"])
                    blk.add("pool", lambda e: e.memset(stb[:, 0], 0.0), writes=[("stb", 0)])
                else:
                    blk.dma("sp", stf[:], self.rst[h], writes=["stf"])
                    blk.add("act", lambda e: e.copy(stb[:, 0], stf[:]), reads=["stf"], writes=[("stb", 0)])
                sdec = float(np.exp(128.0 * np.log1p(-2.0 ** (-5.0 - h))))
                def emit_scores(j, h=h):
                    jsl = slice(j * 128, (j + 1) * 128)
                    sb_ = j % 2
                    for dc in range(2):
                        blk.add("pe", lambda e, dc=dc, jsl=jsl: e.matmul(psS[:, 0:128], kf[:, dc, jsl], qf[:, dc, jsl],
                                                                       start=(dc == 0), stop=(dc == 1)),
                                reads=["k", "q"], writes=["S"])
                    blk.add("dve", lambda e, h=h, sb_=sb_: e.tensor_tensor(sT[:, sb_, :], psS[:, 0:128], cmask[:, h, :], ALU.mult),
                            reads=["S", "cmask"], writes=[("sT", sb_)])

                def emit_transposes(j):
                    jsl = slice(j * 128, (j + 1) * 128)
                    ob = j % 2
                    tr = trot.next()
                    for vc in range(4):
                        blk.add("pe", lambda e, vc=vc, tr=tr, ob=ob: e.transpose(psT[:, tr, vc, :], onb[:, ob, vc * 128:(vc + 1) * 128], ident[:]),
                                reads=[("onb", ob), "ident"], writes=[("T", tr)])
                    blk.add("act", lambda e, tr=tr, jsl=jsl: e.copy(ogs[:, :, jsl], psT[:, tr, 0:4, :]),
                            reads=[("T", tr)], writes=["ogs"])

                emit_scores(0)
                for j in range(NJ):
                    jsl = slice(j * 128, (j + 1) * 128)
                    cur, nxt = j % 2, (j + 1) % 2
                    if j + 1 < NJ:
                        emit_scores(j + 1)
                    for dc in range(2):
                        blk.add("pe", lambda e, j=j, dc=dc: e.matmul(psU[:, dc, :], ktm[:, j, dc * 128:(dc + 1) * 128], vtm[:, j, :],
                                                                   start=True, stop=True),
                                reads=["ktm", "vtm"], writes=[("U", dc)])
                        blk.add("dve", lambda e, dc=dc, sdec=sdec: e.scalar_tensor_tensor(
                            stf[:, dc, :], stf[:, dc, :], sdec, psU[:, dc, :], op0=ALU.mult, op1=ALU.add),
                            reads=[("U", dc), "stf"], writes=["stf"])
                    blk.add("act", lambda e, nxt=nxt: e.copy(stb[:, nxt], stf[:]), reads=["stf"], writes=[("stb", nxt)])
                    pump(10)
                    blk.add("pe", lambda e, j=j, cur=cur: e.matmul(psO[:], sT[:, cur, :], vtm[:, j, :], start=True, stop=False),
                            reads=[("sT", cur), "vtm"], writes=["O"])
                    for dc in range(2):
                        blk.add("pe", lambda e, dc=dc, jsl=jsl, cur=cur: e.matmul(psO[:], qf[:, dc, jsl], stb[:, cur, dc, :],
                                                                                start=False, stop=(dc == 1)),
                                reads=["q", ("stb", cur)], writes=["O"])
                    blk.add("act", lambda e, cur=cur: e.activation(junk[:], psO[:], AF.Square, accum_out=ssp[:, cur, 0:1]),
                            reads=["O"], writes=["junk", ("ssp", cur, 0)])
                    blk.add("dve", lambda e, h=h, cur=cur: e.tensor_scalar(ssp[:, cur, 1:2], ssp[:, cur, 0:1],
                                                                         ccol[:, 8 + 2 * H + h:9 + 2 * H + h], EPS,
                                                                         op0=ALU.mult, op1=ALU.add),
                            reads=[("ssp", cur, 0), "ccol"], writes=[("ssp", cur, 1)])
                    blk.add("act", lambda e, cur=cur: e.activation(ssp[:, cur, 2:3], ssp[:, cur, 1:2], AF.Sqrt),
                            reads=[("ssp", cur, 1)], writes=[("ssp", cur, 2)])
                    blk.add("dve", lambda e, cur=cur: e.reciprocal(ssp[:, cur, 2:3], ssp[:, cur, 2:3]),
                            reads=[("ssp", cur, 2)], writes=[("ssp", cur, 2)])
                    blk.add("dve", lambda e, h=h, cur=cur: e.tensor_tensor(ssp[:, cur, 3:4], ssp[:, cur, 2:3], ccol[:, 8 + h:9 + h], ALU.mult),
                            reads=[("ssp", cur, 2), "ccol"], writes=[("ssp", cur, 3)])
                    blk.add("dve", lambda e, cur=cur: e.tensor_scalar(onb[:, cur, :], psO[:], ssp[:, cur, 3:4], None, op0=ALU.mult),
                            reads=["O", ("ssp", cur, 3)], writes=[("onb", cur)])
                    pump(11)
                    if j >= 1:
                        emit_transposes(j - 1)
                    pump(11)
                emit_transposes(NJ - 1)
                pump(10 ** 9)
                blk.add("dve", lambda e: e.tensor_tensor(ogs[:], ogs[:], sg[:], ALU.mult), reads=["ogs", "sg"], writes=["ogs"])
                blk.dma("sp", self.rst[h], stf[:], reads=["stf"])
                blk.dma("sp", self.og[h * 4:(h + 1) * 4].rearrange("a p t -> p a t"), ogs[:], reads=["ogs"])
            blk.emit()

    @_phase
    def proj_resid(self, name, x_dram, kp, KCn, w_dram, h_src, h_dst, tok_blocks):
        nc, c = self.nc, self.c
        blk = Blk(nc, name)
        with ExitStack() as es:
            TB = tok_blocks[0][1]
            x_fm = es.enter_context(nc.sbuf_tensor(f"{name}_x", [128, KCn, TB], BF16))
            ps = es.enter_context(nc.psum_tensor(f"{name}_ps", [128, 4, 512], F32))
            banks = Rot(4)
            ep = self.resid_epilogue(blk, es, ps, h_src, h_dst)
            nsl = 2 if KCn * 128 * 2 > 16384 else 3
            wsl = es.enter_context(nc.sbuf_tensor(f"{name}_wsl", [128, nsl, KCn, 128], BF16))
            wrot = Rot(nsl)
            G = 8 if KCn % 8 == 0 else 1
            per = KCn // G
            for (B0, BW) in tok_blocks:
                for g in range(G):
                    blk.dma("sp", x_fm[0:kp, g * per:(g + 1) * per, 0:BW],
                            x_dram[g * per:(g + 1) * per, :, B0:B0 + BW].rearrange("a p t -> p a t"),
                            writes=[("x", g)])
                for n in range(c.KC):
                    s = wrot.next()
                    blk.dma("pool", wsl[0:kp, s, :, :], w_dram[n], writes=[("w", s)], max_dma_last_dim=8192)
                    for t0 in range(0, BW, c.TN):
                        tw = min(c.TN, BW - t0)
                        b = banks.next()
                        for kc in range(KCn):
                            blk.add("pe", lambda e, s=s, kc=kc, b=b, t0=t0, tw=tw: e.matmul(
                                ps[:, b, 0:tw], wsl[0:kp, s, kc, :], x_fm[0:kp, kc, t0:t0 + tw],
                                start=(kc == 0), stop=(kc == KCn - 1)),
                                reads=[("w", s), ("x", kc // per)], writes=[("ps", b)])
                        ep(n, B0 + t0, tw, b)
            blk.emit()

    @_phase
    def mlp_up(self, name, y_fm, layer):
        nc, c = self.nc, self.c
        TN, SEG = c.TN, c.SEG
        blk = Blk(nc, name)
        with ExitStack() as es:
            ps = es.enter_context(nc.psum_tensor(f"{name}_ps", [128, 4, 512], F32))
            rl = es.enter_context(nc.sbuf_tensor(f"{name}_rl", [128, 2, TN], F32))
            zs = es.enter_context(nc.sbuf_tensor(f"{name}_zs", [128, 2, SEG], BF16))
            rrot = Rot(2)

            def ep(n, t0, tw, b):
                s = rrot.next()
                zb = n % 2
                blk.add("act", lambda e: e.activation(rl[:, s, 0:tw], ps[:, b, 0:tw], AF.Relu),
                        reads=[("ps", b)], writes=[("rl", s)])
                blk.add("dve", lambda e: e.tensor_tensor(zs[:, zb, t0:t0 + tw], rl[:, s, 0:tw], rl[:, s, 0:tw], ALU.mult),
                        reads=[("rl", s)], writes=[("zs", zb)])
                if t0 + tw == SEG:
                    blk.dma("sp", self.z[n], zs[:, zb, :], reads=[("zs", zb)])

            tn_list = [(t * TN, TN) for t in range(c.NT)]
            self.gemm(blk, es, [self.w_up[layer, n] for n in range(c.FC)], 128, c.KC, 128, y_fm, "y", 4,
                      Rot(4), ps, ep, tn_list)
            blk.emit()

    @_phase
    def lru_a(self, name, y_fm, st, first):
        nc, c = self.nc, self.c
        KC, TN, SEG, LC = c.KC, c.TN, c.SEG, c.LC
        tail, hlast, cst = st
        blk = Blk(nc, name)
        with ExitStack() as es:
            sb = lambda nm, shape, dt=F32: es.enter_context(nc.sbuf_tensor(f"{name}_{nm}", shape, dt))
            lp = sb("lp", [112, LC, 8])
            blk.dma("sp", lp[:], self.lru_p, writes=["lp"])
            if first:
                blk.add("pool", lambda e: e.memset(tail[:], 0.0), writes=["tail"])
                blk.add("pool", lambda e: e.memset(hlast[:], 0.0), writes=["hlast"])
                blk.add("act", lambda e: e.activation(cst[:], lp[:, :, 7], AF.Exp, scale=-1.0), reads=["lp"], writes=["cst"])
                blk.add("dve", lambda e: e.tensor_scalar(cst[:], cst[:], 1.0, None, op0=ALU.add), reads=["cst"], writes=["cst"])
                blk.add("act", lambda e: e.activation(cst[:], cst[:], AF.Ln), reads=["cst"], writes=["cst"])
                blk.add("act", lambda e: e.mul(cst[:], cst[:], -LRU_C), reads=["cst"], writes=["cst"])
            NSL = 4
            wsl = sb("wsl", [128, NSL, KC, 112], BF16)
            wrot = Rot(NSL)
            gw = sb("gw", [112, 2, 2, 3, 336], BF16)
            ps = es.enter_context(nc.psum_tensor(f"{name}_ps", [128, 4, 512], F32))
            psG = es.enter_context(nc.psum_tensor(f"{name}_psG", [128, 4, 512], F32))
            banks = Rot(4)
            gbanks = Rot(4)
            xb = sb("xb", [112, 3, SEG + 3])
            xc = sb("xc", [112, 3, SEG])
            xcb = sb("xcb", [112, 3, SEG], BF16)
            rg = sb("rg", [112, 3, SEG])
            ig = sb("ig", [112, 3, SEG])
            av = sb("av", [112, SEG])
            t1 = sb("t1", [112, SEG])
            t2 = sb("t2", [112, SEG])
            hs = sb("hs", [112, SEG])
            gx = sb("gx", [112, SEG])
            gt = sb("gt", [112, SEG])
            ygs = sb("ygs", [112, 2, SEG], BF16)
            yrot = Rot(2)

            def load_w(n):
                s = wrot.next()
                blk.dma("pool", wsl[:, s, :, :], self.w_li[n], writes=[("w", s)], max_dma_last_dim=8192)
                return s

            def mm_chunk(s, t0, tw):
                b = banks.next()
                for kc in range(KC):
                    blk.add("pe", lambda e, kc=kc: e.matmul(ps[0:112, b, 0:tw], wsl[:, s, kc, :], y_fm[:, kc, t0:t0 + tw],
                                                           start=(kc == 0), stop=(kc == KC - 1)),
                            reads=[("w", s)], writes=[("ps", b)])
                return b

            for n in range(c.LB):
                gs = n % 2
                blk.dma("pool", gw[:, gs], self.w_gt[n], writes=[("gw", gs)], max_dma_last_dim=8192)
                for i in range(3):
                    ch = 3 * n + i
                    s = load_w(ch)
                    blk.add("dve", lambda e, i=i, ch=ch: e.tensor_copy(xb[:, i, 0:3], tail[:, ch, :]),
                            reads=["tail"], writes=[("xb", i)])
                    for tt in range(c.NT):
                        t0 = tt * TN
                        b = mm_chunk(s, t0, TN)
                        blk.add("act", lambda e, i=i, t0=t0, b=b: e.copy(xb[:, i, 3 + t0:3 + t0 + TN], ps[0:112, b, 0:TN]),
                                reads=[("ps", b)], writes=[("xb", i)])
                    blk.add("dve", lambda e, i=i, ch=ch: e.tensor_copy(tail[:, ch, :], xb[:, i, SEG:SEG + 3]),
                            reads=[("xb", i)], writes=["tail"])
                    blk.add("act", lambda e, i=i, ch=ch: e.activation(xc[:, i, :], xb[:, i, 3:3 + SEG], AF.Identity,
                                                                      bias=lp[:, ch, 4:5], scale=lp[:, ch, 3:4]),
                            reads=[("xb", i), "lp"], writes=[("xc", i)])
                    for jj in range(3):
                        blk.add("dve", lambda e, i=i, ch=ch, jj=jj: e.scalar_tensor_tensor(
                            xc[:, i, :], xb[:, i, jj:jj + SEG], lp[:, ch, jj:jj + 1], xc[:, i, :], op0=ALU.mult, op1=ALU.add),
                            reads=[("xb", i), "lp", ("xc", i)], writes=[("xc", i)])
                    blk.add("act", lambda e, i=i: e.copy(xcb[:, i, :], xc[:, i, :]), reads=[("xc", i)], writes=[("xcb", i)])
                for gi, (dst, bcol) in enumerate(((rg, 5), (ig, 6))):
                    for jd in range(3):
                        ch = 3 * n + jd
                        for tt in range(c.NT):
                            t0 = tt * TN
                            b = gbanks.next()
                            for i in range(3):
                                blk.add("pe", lambda e, gi=gi, jd=jd, i=i, b=b, t0=t0, gs=gs: e.matmul(
                                    psG[0:112, b, 0:TN], gw[:, gs, gi, i, jd * 112:(jd + 1) * 112], xcb[:, i, t0:t0 + TN],
                                    start=(i == 0), stop=(i == 2)),
                                    reads=[("gw", gs), ("xcb", i)], writes=[("psG", b)])
                            blk.add("act", lambda e, dst=dst, jd=jd, ch=ch, bcol=bcol, b=b, t0=t0: e.activation(
                                dst[:, jd, t0:t0 + TN], psG[0:112, b, 0:TN], AF.Sigmoid, bias=lp[:, ch, bcol:bcol + 1]),
                                reads=[("psG", b), "lp"], writes=[("gate", gi, jd)])
                for i in range(3):
                    ch = 3 * n + i
                    blk.add("act", lambda e, i=i, ch=ch: e.activation(av[:], rg[:, i, :], AF.Exp, scale=cst[:, ch:ch + 1]),
                            reads=[("gate", 0, i), "cst"], writes=["av"])
                    blk.add("dve", lambda e: e.tensor_tensor(t1[:], av[:], av[:], ALU.mult), reads=["av"], writes=["t1"])
                    blk.add("dve", lambda e: e.tensor_scalar(t1[:], t1[:], -1.0, 1.0, op0=ALU.mult, op1=ALU.add),
                            reads=["t1"], writes=["t1"])
                    blk.add("dve", lambda e: e.tensor_scalar(t1[:], t1[:], 1e-30, None, op0=ALU.max),
                            reads=["t1"], writes=["t1"])
                    blk.add("act", lambda e: e.activation(t1[:], t1[:], AF.Sqrt), reads=["t1"], writes=["t1"])
                    blk.add("dve", lambda e, i=i: e.tensor_tensor(t2[:], ig[:, i, :], xc[:, i, :], ALU.mult),
                            reads=[("gate", 1, i), ("xc", i)], writes=["t2"])
                    blk.add("dve", lambda e: e.tensor_tensor(t2[:], t2[:], t1[:], ALU.mult), reads=["t1", "t2"], writes=["t2"])
                    blk.add("dve", lambda e, ch=ch: e.tensor_tensor_scan(hs[:], av[:], t2[:], hlast[:, ch:ch + 1],
                                                                      op0=ALU.mult, op1=ALU.add),
                            reads=["av", "t2", "hlast"], writes=["hs"])
                    blk.add("dve", lambda e, ch=ch: e.tensor_copy(hlast[:, ch:ch + 1], hs[:, SEG - 1:SEG]),
                            reads=["hs"], writes=["hlast"])
                    s = load_w(LC + ch)
                    for tt in range(c.NT):
                        t0 = tt * TN
                        b = mm_chunk(s, t0, TN)
                        blk.add("act", lambda e, t0=t0, b=b: e.copy(gx[:, t0:t0 + TN], ps[0:112, b, 0:TN]),
                                reads=[("ps", b)], writes=["gx"])
                    blk.add("dve", lambda e: e.tensor_tensor(gt[:], gx[:], gx[:], ALU.mult), reads=["gx"], writes=["gt"])
                    blk.add("dve", lambda e: e.tensor_scalar(gt[:], gt[:], 0.044715, 1.0, op0=ALU.mult, op1=ALU.add),
                            reads=["gt"], writes=["gt"])
                    blk.add("dve", lambda e: e.tensor_tensor(gt[:], gt[:], gx[:], ALU.mult), reads=["gt", "gx"], writes=["gt"])
                    blk.add("act", lambda e: e.activation(gt[:], gt[:], AF.Sigmoid, scale=1.5957691216057308),
                            reads=["gt"], writes=["gt"])
                    blk.add("dve", lambda e: e.tensor_tensor(gt[:], gt[:], gx[:], ALU.mult), reads=["gt", "gx"], writes=["gt"])
                    ys = yrot.next()
                    blk.add("dve", lambda e, ys=ys: e.tensor_tensor(ygs[:, ys, :], hs[:], gt[:], ALU.mult),
                            reads=["hs", "gt"], writes=[("ygs", ys)])
                    blk.dma("sp", self.yg[ch], ygs[:, ys, :], reads=[("ygs", ys)])
            blk.emit()

    def build(self, max_phase=10**9):
        nc, c = self.nc, self.c
        self._ph = 0
        self._maxph = max_phase
        with ExitStack() as top:
            tail = top.enter_context(nc.sbuf_tensor("lru_tail", [112, c.LC, 3], F32))
            hlast = top.enter_context(nc.sbuf_tensor("lru_hlast", [112, c.LC], F32))
            cst = top.enter_context(nc.sbuf_tensor("lru_cst", [112, c.LC], F32))
            full = [(0, c.SEG)]
            zblocks = [(t * c.TN, c.TN) for t in range(c.NT)]

            def with_y(fn):
                with nc.sbuf_tensor(f"y_fm_{nc.next_id()}", [128, c.KC, c.SEG], BF16) as y_fm:
                    fn(y_fm)

            for seg in range(c.NSEG):
                first = (seg % c.SPS == 0)
                p = f"s{seg}"

                def l0(y_fm):
                    self.norm(p + "n0", self.xT[seg], 0, y_fm=y_fm)
                    self.ret_a(p + "ra", y_fm, seg, first)
                with_y(l0)
                self.proj_resid(p + "ro", self.og, 128, c.H * 4, self.w_ro, self.xT[seg], self.hA, full)

                def m0(y_fm):
                    self.norm(p + "n1", self.hA, 2, y_fm=y_fm)
                    self.mlp_up(p + "u0", y_fm, 0)
                with_y(m0)
                self.proj_resid(p + "d0", self.z, 128, c.FC, self.w_dn[0], self.hA, self.hB, zblocks)

                def l1(y_fm):
                    self.norm(p + "n2", self.hB, 1, y_fm=y_fm)
                    self.lru_a(p + "la", y_fm, (tail, hlast, cst), first)
                with_y(l1)
                self.proj_resid(p + "lo", self.yg, 112, c.LC, self.w_lo, self.hB, self.hC, full)

                def m1(y_fm):
                    self.norm(p + "n3", self.hC, 3, y_fm=y_fm)
                    self.mlp_up(p + "u1", y_fm, 1)
                with_y(m1)
                self.proj_resid(p + "d1", self.z, 128, c.FC, self.w_dn[1], self.hC, self.hD, zblocks)
                self.norm(p + "nf", self.hD, 4, out_dram=self.outT[seg])
        return nc


def _tile_w(W, kp, M, colperm=None):
    K, N = W.shape
    if colperm is not None:
        W = W[:, colperm]
    return np.ascontiguousarray(W.reshape(K // kp, kp, N // M, M).transpose(2, 1, 0, 3))


def _consts(cfg):
    H = cfg.H
    k = np.arange(128)[:, None].astype(np.float64)
    q = np.arange(128)[None, :].astype(np.float64)
    cmask = np.zeros((128, H, 128), np.float32)
    ccol = np.zeros((128, 8 + 3 * H), np.float32)
    ccol[:, 0] = (np.float32(ROPE_THETA) ** (-(np.arange(128, dtype=np.float32) / np.float32(128)))).astype(np.float32)
    ccol[:, 1] = np.float32(np.pi)
    t = np.arange(128, dtype=np.float64)
    for h in range(H):
        lg = np.log1p(-2.0 ** (-5.0 - h))
        m = np.where((k // 64) <= (q // 64), np.exp(lg * (np.abs(q - k) - (q + 1.0))), 0.0)
        cmask[:, h, :] = m.astype(np.float32)
        qdec = np.exp(lg * (t + 1.0))
        ccol[:, 8 + h] = qdec
        ccol[:, 8 + H + h] = np.exp(lg * (127.0 - t))
        ccol[:, 8 + 2 * H + h] = qdec * qdec / cfg.DV
    return cmask, ccol, np.eye(128, dtype=np.float32)


def prepare_inputs(cfg, x, positions, norm_mix_g, norm_mlp_g, final_norm_g, ret_w_in, ret_w_out,
                   lru_w_in, lru_conv_w, lru_conv_b, lru_w_rgate, lru_b_rgate, lru_w_igate,
                   lru_b_igate, lru_lambda, lru_w_out, mlp_w_up, mlp_w_down):
    c = cfg
    f = lambda a: np.asarray(a, dtype=np.float32)
    D, H, LC, LW, KC = c.D, c.H, c.LC, c.LW, c.KC
    shared = {}
    gl = np.stack([f(norm_mix_g)[0], f(norm_mix_g)[1], f(norm_mlp_g)[0], f(norm_mlp_g)[1], f(final_norm_g)], 0)
    shared["gains"] = np.ascontiguousarray(gl.reshape(5, KC, 128).transpose(2, 0, 1))
    perm = []
    for h in range(H):
        perm += list(range(h * 256, (h + 1) * 256))
        perm += list(range(D + h * 256, D + (h + 1) * 256))
        perm += list(range(2 * D + h * 512, 2 * D + (h + 1) * 512))
        perm += list(range(2 * D + c.HV + h * 512, 2 * D + c.HV + (h + 1) * 512))
    shared["w_in"] = _tile_w(f(ret_w_in)[0], 128, 128, np.asarray(perm))
    shared["w_ro"] = _tile_w(f(ret_w_out)[0], 128, 128)
    shared["w_li"] = _tile_w(f(lru_w_in)[0], 128, 112)
    shared["w_lo"] = _tile_w(f(lru_w_out)[0], 112, 128)
    shared["w_up"] = np.stack([_tile_w(f(mlp_w_up)[l], 128, 128) for l in range(2)], 0)
    shared["w_dn"] = np.stack([_tile_w(f(mlp_w_down)[l], 128, 128) for l in range(2)], 0)
    wr, wi = f(lru_w_rgate)[0], f(lru_w_igate)[0]
    wg = np.stack([wr, wi], 1)
    shared["w_gt"] = np.ascontiguousarray(wg.reshape(c.LB, 2, 3, 112, 336).transpose(0, 3, 1, 2, 4))
    lp = np.zeros((112, LC, 8), np.float32)
    cw = f(lru_conv_w)[0]
    for j in range(4):
        lp[:, :, j] = cw[j].reshape(LC, 112).T
    lp[:, :, 4] = f(lru_conv_b)[0].reshape(LC, 112).T
    lp[:, :, 5] = f(lru_b_rgate)[0].reshape(LC, 112).T
    lp[:, :, 6] = f(lru_b_igate)[0].reshape(LC, 112).T
    lp[:, :, 7] = f(lru_lambda)[0].reshape(LC, 112).T
    shared["lru_p"] = lp
    shared["cmask"], shared["ccol"], shared["ident"] = _consts(c)
    xs = f(x)
    pos = np.asarray(positions).astype(np.int32)
    in_maps = []
    for core in range(c.NCORES):
        xb = xs[core * c.SPC:(core + 1) * c.SPC]
        xT = np.ascontiguousarray(xb.reshape(c.SPC * c.SPS, c.SEG, D).transpose(0, 2, 1))
        pp = np.ascontiguousarray(pos[core * c.SPC:(core + 1) * c.SPC].reshape(c.NSEG, c.SEG))
        m = dict(shared)
        m["xT"] = xT
        m["pos"] = pp
        in_maps.append(m)
    return in_maps


def run(cfg, inputs, debug=False, trace=False, max_phase=10**9):
    b = Builder(cfg, debug=debug)
    nc = b.build(max_phase)
    in_maps = prepare_inputs(cfg, **inputs)
    res = run_bass_kernel_spmd(nc, in_maps, core_ids=list(range(cfg.NCORES)), **({"trace": True} if trace else {}))
    outs = []
    for core in range(cfg.NCORES):
        oT = res.results[core]["outT"]
        outs.append(oT.transpose(0, 2, 1).reshape(cfg.SPC, cfg.SEQ, cfg.D))
    out = np.ascontiguousarray(np.concatenate(outs, 0)).astype(np.float32)
    return out, res


def kernel(**inputs):
    cfg = Cfg()
    out, _ = run(cfg, inputs)
    return out
```

```python
import os
import numpy as np
from contextlib import ExitStack
import concourse.bass as bass
import concourse.mybir as mybir
from concourse.bass_utils import run_bass_kernel_spmd

F32 = mybir.dt.float32
BF16 = mybir.dt.bfloat16
I32 = mybir.dt.int32
AF = mybir.ActivationFunctionType
ALU = mybir.AluOpType

EPS = 1e-6
LRU_C = 8.0
ROPE_THETA = 10000.0


class Cfg:
    def __init__(self, D=4096, SEQ=4096, B=4, H=16, LB=16, FF=16384, SEG=1024, NCORES=4):
        self.D, self.SEQ, self.B, self.H, self.LB, self.FF, self.SEG, self.NCORES = D, SEQ, B, H, LB, FF, SEG, NCORES
        self.DK, self.DV = 256, 512
        self.HV = H * self.DV
        self.LBS = 336
        self.LW = LB * self.LBS
        self.LC = self.LW // 112
        self.KC = D // 128
        self.FC = FF // 128
        self.TN = min(512, SEG)
        self.NT = SEG // self.TN
        self.NJ = SEG // 128
        self.SPC = B // NCORES
        self.SPS = SEQ // SEG
        self.NSEG = self.SPC * self.SPS
        assert D == H * self.DK and SEG % 128 == 0 and SEQ % SEG == 0 and B % NCORES == 0


ENGS = ("pe", "act", "dve", "pool", "sp")
BLOCK_ATTR = {"pe": "tensor", "act": "scalar", "dve": "vector", "pool": "gpsimd", "sp": "sync"}
NDSEM = 8


class Op:
    __slots__ = ("eng", "fn", "deps", "sig", "is_dma", "tok", "pre")

    def __init__(self, eng, fn, is_dma):
        self.eng, self.fn, self.is_dma = eng, fn, is_dma
        self.deps = ()
        self.sig = is_dma
        self.tok = None
        self.pre = None


class Blk:
    def __init__(self, nc, name):
        self.nc, self.name = nc, name
        self.ops = {e: [] for e in ENGS}
        self.last_w = {}
        self.readers = {}

    def add(self, eng, fn, reads=(), writes=(), dma=False):
        op = Op(eng, fn, dma)
        deps = set()
        for k in reads:
            w = self.last_w.get(k)
            if w is not None:
                deps.add(w)
        for k in writes:
            w = self.last_w.get(k)
            if w is not None:
                deps.add(w)
            for r in self.readers.get(k, ()):
                deps.add(r)
        if eng == "pe":
            deps = {d for d in deps if d.eng != "pe"}
        for d in deps:
            d.sig = True
        op.deps = tuple(deps)
        for k in reads:
            lst = self.readers.setdefault(k, [])
            if not dma:
                lst[:] = [r for r in lst if r.is_dma or r.eng != eng]
            lst.append(op)
        for k in writes:
            self.last_w[k] = op
            self.readers[k] = []
        self.ops[eng].append(op)
        return op

    def dma(self, q, out, in_, reads=(), writes=(), **kw):
        return self.add(q, lambda e: e.dma_start(out=out, in_=in_, **kw), reads, writes, dma=True)

    def emit(self):
        nc = self.nc
        sems = {}
        for e in ENGS:
            if e != "sp" and any(not o.is_dma for o in self.ops[e]):
                sems[("c", e)] = nc.alloc_semaphore(name=f"{self.name}_c{e}")
            if any(o.is_dma for o in self.ops[e]):
                for i in range(NDSEM):
                    sems[("d", e, i)] = nc.alloc_semaphore(name=f"{self.name}_d{e}{i}")
        finals = {e: {} for e in ENGS}
        for e in ENGS:
            cnt = 0
            k = 0
            dcount = [0] * NDSEM
            for op in self.ops[e]:
                if op.is_dma:
                    i = k % NDSEM
                    k += 1
                    if dcount[i]:
                        op.pre = (("d", e, i), dcount[i])
                    dcount[i] += 16
                    op.tok = (("d", e, i), dcount[i])
                    finals[e][("d", e, i)] = dcount[i]
                elif op.sig:
                    cnt += 1
                    op.tok = (("c", e), cnt)
        with nc.Block() as block:
            for e in ENGS:
                if not self.ops[e]:
                    continue

                def body(eng, e=e):
                    known = {}
                    for op in self.ops[e]:
                        waits = {}
                        for d in op.deps:
                            s, v = d.tok
                            if waits.get(s, 0) < v:
                                waits[s] = v
                        if op.pre is not None:
                            s, v = op.pre
                            if waits.get(s, 0) < v:
                                waits[s] = v
                        for s, v in waits.items():
                            if known.get(s, 0) < v:
                                eng.wait_ge(sems[s], v)
                                known[s] = v
                        ins = op.fn(eng)
                        if op.tok is not None:
                            ins.then_inc(sems[op.tok[0]], 16 if op.is_dma else 1)
                    for s, v in finals[e].items():
                        if known.get(s, 0) < v:
                            eng.wait_ge(sems[s], v)

                getattr(block, BLOCK_ATTR[e])(body)
        nc.clear_and_free_semaphores(list(sems.values()))
        nc.all_engine_barrier()


class Rot:
    def __init__(self, n):
        self.n, self.i = n, 0

    def next(self):
        r = self.i % self.n
        self.i += 1
        return r


def _phase(fn):
    def wrapped(self, *a, **k):
        self._ph += 1
        if self._ph > self._maxph:
            return
        return fn(self, *a, **k)
    return wrapped


class Builder:
    def __init__(self, cfg, debug=False):
        self.c = c = cfg
        self.debug = debug
        self.nc = nc = bass.Bass("TRN2", target_bir_lowering=False)
        ein = lambda name, shape, dt=F32: nc.dram_tensor(name, list(shape), dt, kind="ExternalInput").ap()
        KC, FC, LC, H = c.KC, c.FC, c.LC, c.H
        self.xT = ein("xT", [c.NSEG, c.D, c.SEG])
        self.pos = ein("pos", [c.NSEG, c.SEG], I32)
        self.gains = ein("gains", [128, 5, KC])
        self.w_in = ein("w_in", [H * 12, 128, KC, 128])
        self.w_ro = ein("w_ro", [KC, 128, H * 4, 128])
        self.w_li = ein("w_li", [2 * LC, 128, KC, 112])
        self.w_lo = ein("w_lo", [KC, 112, LC, 128])
        self.w_up = ein("w_up", [2, FC, 128, KC, 128])
        self.w_dn = ein("w_dn", [2, KC, 128, FC, 128])
        self.w_gt = ein("w_gt", [c.LB, 112, 2, 3, 336])
        self.lru_p = ein("lru_p", [112, LC, 8])
        self.cmask = ein("cmask", [128, H, 128])
        self.ccol = ein("ccol", [128, 8 + 3 * H])
        self.ident = ein("ident", [128, 128])
        self.outT = nc.dram_tensor("outT", [c.NSEG, c.D, c.SEG], F32, kind="ExternalOutput").ap()
        kind = "ExternalOutput" if debug else "Internal"
        scr = lambda name, shape, dt=F32: nc.dram_tensor(name, list(shape), dt, kind=kind).ap()
        self.hA = scr("hA", [c.D, c.SEG])
        self.hB = scr("hB", [c.D, c.SEG])
        self.hC = scr("hC", [c.D, c.SEG])
        self.hD = scr("hD", [c.D, c.SEG])
        self.og = scr("og", [H * 4, 128, c.SEG], BF16)
        self.z = scr("z", [FC, 128, c.SEG], BF16)
        self.yg = scr("yg", [LC, 112, c.SEG], BF16)
        self.rst = scr("rst", [H, 128, 2, 512])
        self.dbg_y = scr("dbg_y", [c.D, c.SEG], BF16)

    def gemm(self, blk, es, w_chunks, kp, KCn, M, x_fm, xkey, nslots, banks, ps, epilogue, tn_list, wname="w"):
        nc = self.nc
        wsl = es.enter_context(nc.sbuf_tensor(f"{blk.name}_{wname}sl", [128, nslots, KCn, M], BF16))
        rot = Rot(nslots)
        for n, wd in enumerate(w_chunks):
            s = rot.next()
            blk.dma("pool", wsl[0:kp, s, :, :], wd, writes=[(wname, s)], max_dma_last_dim=8192)
            for (t0, tw) in tn_list:
                b = banks.next()
                for kc in range(KCn):
                    blk.add("pe", lambda e, s=s, kc=kc, b=b, t0=t0, tw=tw: e.matmul(
                        ps[0:M, b, 0:tw], wsl[0:kp, s, kc, :], x_fm[0:kp, kc, t0:t0 + tw],
                        start=(kc == 0), stop=(kc == KCn - 1)),
                        reads=[(wname, s), xkey], writes=[("ps", b)])
                epilogue(n, t0, tw, b)

    def consts(self, blk, es, want=("gains",)):
        nc, c = self.nc, self.c
        out = {}
        if "gains" in want:
            g = es.enter_context(nc.sbuf_tensor(f"{blk.name}_gains", [128, 5, c.KC], F32))
            blk.dma("sp", g[:], self.gains, writes=["gains"])
            out["gains"] = g
        if "ccol" in want:
            t = es.enter_context(nc.sbuf_tensor(f"{blk.name}_ccol", [128, 8 + 3 * c.H], F32))
            blk.dma("sp", t[:], self.ccol, writes=["ccol"])
            out["ccol"] = t
        if "ident" in want:
            t = es.enter_context(nc.sbuf_tensor(f"{blk.name}_ident", [128, 128], BF16))
            blk.dma("pool", t[:], self.ident, writes=["ident"])
            out["ident"] = t
        if "cmask" in want:
            t = es.enter_context(nc.sbuf_tensor(f"{blk.name}_cmask", [128, c.H, 128], F32))
            blk.dma("sp", t[:], self.cmask, writes=["cmask"])
            out["cmask"] = t
        return out

    @_phase
    def norm(self, name, h_src, gidx, y_fm=None, out_dram=None):
        nc, c = self.nc, self.c
        KC, TN = c.KC, c.TN
        blk = Blk(nc, name)
        with ExitStack() as es:
            ht = es.enter_context(nc.sbuf_tensor(f"{name}_ht", [128, 2, KC, TN], F32))
            sq = es.enter_context(nc.sbuf_tensor(f"{name}_sq", [128, 2, TN], BF16))
            rstd = es.enter_context(nc.sbuf_tensor(f"{name}_rstd", [128, 2, TN], F32))
            ones = es.enter_context(nc.sbuf_tensor(f"{name}_ones", [128, 128], BF16))
            ps = es.enter_context(nc.psum_tensor(f"{name}_ps", [128, 2, 512], F32))
            ost = None
            if out_dram is not None:
                ost = es.enter_context(nc.sbuf_tensor(f"{name}_ost", [128, 2, TN], F32))
            cs = self.consts(blk, es, ("gains",))
            gains = cs["gains"]
            blk.add("pool", lambda e: e.memset(ones[:], 1.0), writes=["ones"])
            hv = h_src.rearrange("(kc p) t -> p kc t", p=128)
            G = 4 if KC >= 4 else 1
            per = KC // G
            for tt in range(c.NT):
                t0 = tt * TN
                hb = tt % 2
                for g in range(G):
                    blk.dma("sp", ht[:, hb, g * per:(g + 1) * per, :], hv[:, g * per:(g + 1) * per, t0:t0 + TN],
                            writes=[("ht", hb, g)])
                b = tt % 2
                for kc in range(KC):
                    s = kc % 2
                    blk.add("act", lambda e, kc=kc, s=s, hb=hb: e.activation(sq[:, s, :], ht[:, hb, kc, :], AF.Square),
                            reads=[("ht", hb, kc // per)], writes=[("sq", s)])
                    blk.add("pe", lambda e, kc=kc, s=s, b=b: e.matmul(ps[:, b, 0:TN], ones[:], sq[:, s, :],
                                                                     start=(kc == 0), stop=(kc == KC - 1)),
                            reads=["ones", ("sq", s)], writes=[("ps", b)])
                blk.add("dve", lambda e, b=b, hb=hb: e.tensor_scalar(rstd[:, hb, :], ps[:, b, 0:TN], 1.0 / c.D, EPS,
                                                                   op0=ALU.mult, op1=ALU.add),
                        reads=[("ps", b)], writes=[("rstd", hb)])
                blk.add("act", lambda e, hb=hb: e.activation(rstd[:, hb, :], rstd[:, hb, :], AF.Sqrt),
                        reads=[("rstd", hb)], writes=[("rstd", hb)])
                blk.add("dve", lambda e, hb=hb: e.reciprocal(rstd[:, hb, :], rstd[:, hb, :]),
                        reads=[("rstd", hb)], writes=[("rstd", hb)])
                for kc in range(KC):
                    if y_fm is not None:
                        blk.add("dve", lambda e, kc=kc, t0=t0, hb=hb: e.scalar_tensor_tensor(
                            y_fm[:, kc, t0:t0 + TN], ht[:, hb, kc, :], gains[:, gidx, kc:kc + 1], rstd[:, hb, :],
                            op0=ALU.mult, op1=ALU.mult),
                            reads=[("ht", hb, kc // per), ("rstd", hb), "gains"], writes=["y"])
                    else:
                        s = kc % 2
                        blk.add("dve", lambda e, kc=kc, s=s, hb=hb: e.scalar_tensor_tensor(
                            ost[:, s, :], ht[:, hb, kc, :], gains[:, gidx, kc:kc + 1], rstd[:, hb, :],
                            op0=ALU.mult, op1=ALU.mult),
                            reads=[("ht", hb, kc // per), ("rstd", hb), "gains"], writes=[("ost", s)])
                        blk.dma("sp", out_dram[kc * 128:(kc + 1) * 128, t0:t0 + TN], ost[:, s, :],
                                reads=[("ost", s)])
            if self.debug and y_fm is not None:
                blk.dma("sp", self.dbg_y.rearrange("(kc p) t -> p kc t", p=128), y_fm[:], reads=["y"])
            blk.emit()

    def resid_epilogue(self, blk, es, ps, h_src, h_dst, nbuf=2):
        nc, c = self.nc, self.c
        TN = c.TN
        hin = es.enter_context(nc.sbuf_tensor(f"{blk.name}_hin", [128, nbuf, TN], F32))
        hout = es.enter_context(nc.sbuf_tensor(f"{blk.name}_hout", [128, nbuf, TN], F32))
        rot = Rot(nbuf)

        def ep(n, t0, tw, b):
            s = rot.next()
            blk.dma("sp", hin[:, s, 0:tw], h_src[n * 128:(n + 1) * 128, t0:t0 + tw], writes=[("hin", s)])
            blk.add("dve", lambda e: e.tensor_tensor(hout[:, s, 0:tw], ps[:, b, 0:tw], hin[:, s, 0:tw], ALU.add),
                    reads=[("ps", b), ("hin", s)], writes=[("hout", s)])
            blk.dma("sp", h_dst[n * 128:(n + 1) * 128, t0:t0 + tw], hout[:, s, 0:tw], reads=[("hout", s)])
        return ep

    @_phase
    def ret_a(self, name, y_fm, seg, first):
        nc, c = self.nc, self.c
        KC, TN, SEG, NJ, H = c.KC, c.TN, c.SEG, c.NJ, c.H
        blk = Blk(nc, name)
        with ExitStack() as es:
            sb = lambda nm, shape, dt=F32: es.enter_context(nc.sbuf_tensor(f"{name}_{nm}", shape, dt))
            cs = self.consts(blk, es, ("ccol", "ident", "cmask"))
            ccol, ident, cmask = cs["ccol"], cs["ident"], cs["cmask"]
            posi = sb("posi", [128, SEG], I32)
            ang = sb("ang", [128, SEG])
            tmp = sb("tmp", [128, SEG])
            tmp2 = sb("tmp2", [128, SEG])
            tabs = sb("tabs", [128, 4, SEG])
            blk.dma("sp", posi[:], self.pos[seg].partition_broadcast(128), writes=["posi"])
            blk.add("dve", lambda e: e.tensor_copy(ang[:], posi[:]), reads=["posi"], writes=["ang"])
            blk.add("dve", lambda e: e.tensor_scalar(ang[:], ang[:], ccol[:, 0:1], None, op0=ALU.mult),
                    reads=["ang", "ccol"], writes=["ang"])
            blk.add("dve", lambda e: e.tensor_scalar(ang[:], ang[:], float(1.0 / (2 * np.pi)), None, op0=ALU.mult),
                    reads=["ang"], writes=["ang"])
            for ti, off in ((1, 0.0), (0, 0.25)):
                blk.add("dve", lambda e, off=off: e.tensor_scalar(tmp[:], ang[:], off, None, op0=ALU.add),
                        reads=["ang"], writes=["tmp"])
                blk.add("dve", lambda e: e.tensor_copy(posi[:], tmp[:]), reads=["tmp"], writes=["posi"])
                blk.add("dve", lambda e: e.tensor_copy(tmp2[:], posi[:]), reads=["posi"], writes=["tmp2"])
                blk.add("dve", lambda e: e.tensor_tensor(tmp[:], tmp[:], tmp2[:], ALU.subtract),
                        reads=["tmp", "tmp2"], writes=["tmp"])
                blk.add("dve", lambda e: e.tensor_scalar(tmp2[:], tmp[:], 0.0, None, op0=ALU.is_lt),
                        reads=["tmp"], writes=["tmp2"])
                blk.add("dve", lambda e: e.tensor_tensor(tmp[:], tmp[:], tmp2[:], ALU.add),
                        reads=["tmp", "tmp2"], writes=["tmp"])
                blk.add("act", lambda e, ti=ti: e.activation(tabs[:, ti, :], tmp[:], AF.Sin, bias=ccol[:, 1:2],
                                                             scale=float(-2 * np.pi)),
                        reads=["tmp", "ccol"], writes=["tabs"])
            blk.add("act", lambda e: e.mul(tabs[:, 2:4, :], tabs[:, 0:2, :], float(c.DK ** -0.5)),
                    reads=["tabs"], writes=["tabs"])

            NSL = 4
            wsl = sb("wsl", [128, NSL, KC, 128], BF16)
            wrot = Rot(NSL)
            ps = es.enter_context(nc.psum_tensor(f"{name}_ps", [128, 2, 512], F32))
            psT = es.enter_context(nc.psum_tensor(f"{name}_psT", [128, 2, 8, 128], BF16))
            psS = es.enter_context(nc.psum_tensor(f"{name}_psS", [128, 512], F32))
            psO = es.enter_context(nc.psum_tensor(f"{name}_psO", [128, 512], F32))
            psU = es.enter_context(nc.psum_tensor(f"{name}_psU", [128, 2, 512], F32))
            banks = Rot(2)
            trot = Rot(2)
            qf = sb("qf", [128, 2, SEG], BF16)
            kf = sb("kf", [128, 2, SEG], BF16)
            ktm = sb("ktm", [128, NJ, 256], BF16)
            vf = sb("vf", [128, 4, SEG], BF16)
            vtm = sb("vtm", [128, NJ, 512], BF16)
            sg = sb("sg", [128, 4, SEG], BF16)
            rt = sb("rt", [128, 4, TN])
            sT = sb("sT", [128, 2, 128], BF16)
            junk = sb("junk", [128, 512], BF16)
            onb = sb("onb", [128, 2, 512], BF16)
            ogs = sb("ogs", [128, 4, SEG], BF16)
            stf = sb("stf", [128, 2, 512])
            stb = sb("stb", [128, 2, 2, 512], BF16)
            ssp = sb("ssp", [128, 2, 4])

            def load_w(n):
                s = wrot.next()
                blk.dma("pool", wsl[:, s, :, :], self.w_in[n], writes=[("w", s)], max_dma_last_dim=8192)
                return s

            def mm_chunk(s, t0, tw):
                b = banks.next()
                for kc in range(KC):
                    blk.add("pe", lambda e, kc=kc: e.matmul(ps[:, b, 0:tw], wsl[:, s, kc, :], y_fm[:, kc, t0:t0 + tw],
                                                           start=(kc == 0), stop=(kc == KC - 1)),
                            reads=[("w", s)], writes=[("ps", b)])
                return b

            for h in range(H):
                for which, dst, ci, si in (("q", qf, 0, 1), ("k", kf, 2, 3)):
                    base = h * 12 + (0 if which == "q" else 2)
                    s0 = load_w(base)
                    s1 = load_w(base + 1)
                    for tt in range(c.NT):
                        t0 = tt * TN
                        bA = mm_chunk(s0, t0, TN)
                        bB = mm_chunk(s1, t0, TN)
                        A, Bp = ps[:, bA, 0:TN], ps[:, bB, 0:TN]
                        cos, sin = tabs[:, ci, t0:t0 + TN], tabs[:, si, t0:t0 + TN]
                        blk.add("dve", lambda e, A=A, cos=cos: e.tensor_tensor(rt[:, 0, :], A, cos, ALU.mult),
                                reads=[("ps", bA), "tabs"], writes=[("rt", 0)])
                        blk.add("dve", lambda e, Bp=Bp, sin=sin: e.tensor_tensor(rt[:, 1, :], Bp, sin, ALU.mult),
                                reads=[("ps", bB), "tabs"], writes=[("rt", 1)])
                        blk.add("dve", lambda e, dst=dst, t0=t0: e.tensor_tensor(dst[:, 0, t0:t0 + TN], rt[:, 0, :], rt[:, 1, :], ALU.subtract),
                                reads=[("rt", 0), ("rt", 1)], writes=[which])
                        blk.add("dve", lambda e, Bp=Bp, cos=cos: e.tensor_tensor(rt[:, 2, :], Bp, cos, ALU.mult),
                                reads=[("ps", bB), "tabs"], writes=[("rt", 2)])
                        blk.add("dve", lambda e, A=A, sin=sin: e.tensor_tensor(rt[:, 3, :], A, sin, ALU.mult),
                                reads=[("ps", bA), "tabs"], writes=[("rt", 3)])
                        blk.add("dve", lambda e, dst=dst, t0=t0: e.tensor_tensor(dst[:, 1, t0:t0 + TN], rt[:, 2, :], rt[:, 3, :], ALU.add),
                                reads=[("rt", 2), ("rt", 3)], writes=[which])
                for j in range(NJ):
                    tr = trot.next()
                    for dc in range(2):
                        blk.add("pe", lambda e, j=j, dc=dc, tr=tr: e.transpose(psT[:, tr, dc, :], kf[:, dc, j * 128:(j + 1) * 128], ident[:]),
                                reads=["k", "ident"], writes=[("T", tr)])
                    blk.add("dve", lambda e, j=j, tr=tr, h=h: e.tensor_scalar(
                        ktm[:, j, :].rearrange("p (a b) -> p a b", a=2), psT[:, tr, 0:2, :], ccol[:, 8 + H + h:9 + H + h], None, op0=ALU.mult),
                        reads=[("T", tr), "ccol"], writes=["ktm"])
                for vc in range(4):
                    s = load_w(h * 12 + 4 + vc)
                    for tt in range(c.NT):
                        t0 = tt * TN
                        b = mm_chunk(s, t0, TN)
                        blk.add("act", lambda e, vc=vc, t0=t0, b=b: e.copy(vf[:, vc, t0:t0 + TN], ps[:, b, 0:TN]),
                                reads=[("ps", b)], writes=["vf"])
                for j in range(NJ):
                    tr = trot.next()
                    for vc in range(4):
                        blk.add("pe", lambda e, j=j, vc=vc, tr=tr: e.transpose(psT[:, tr, vc, :], vf[:, vc, j * 128:(j + 1) * 128], ident[:]),
                                reads=["vf", "ident"], writes=[("T", tr)])
                    blk.add("act", lambda e, j=j, tr=tr: e.copy(vtm[:, j, :].rearrange("p (a b) -> p a b", a=4), psT[:, tr, 0:4, :]),
                            reads=[("T", tr)], writes=["vtm"])
                def g_gen(h=h):
                    for vc in range(4):
                        s = load_w(h * 12 + 8 + vc)
                        for tt in range(c.NT):
                            t0 = tt * TN
                            b = banks.next()
                            for kc in range(KC):
                                blk.add("pe", lambda e, kc=kc, b=b, s=s, t0=t0: e.matmul(
                                    ps[:, b, 0:TN], wsl[:, s, kc, :], y_fm[:, kc, t0:t0 + TN],
                                    start=(kc == 0), stop=(kc == KC - 1)),
                                    reads=[("w", s)], writes=[("ps", b)])
                                yield
                            blk.add("act", lambda e, vc=vc, t0=t0, b=b: e.activation(sg[:, vc, t0:t0 + TN], ps[:, b, 0:TN], AF.Silu),
                                    reads=[("ps", b)], writes=["sg"])
                gen = g_gen()

                def pump(n):
                    for _ in range(n):
                        try:
                            next(gen)
                        except StopIteration:
                            return
                if first:
                    blk.add("pool", lambda e: e.memset(stf[:], 0.0), writes=["stf"])
                    blk.add("pool", lambda e: e.memset(stb[:, 0], 0.0), writes=[("stb", 0)])
                else:
                    blk.dma("sp", stf[:], self.rst[h], writes=["stf"])
                    blk.add("act", lambda e: e.copy(stb[:, 0], stf[:]), reads=["stf"], writes=[("stb", 0)])
                sdec = float(np.exp(128.0 * np.log1p(-2.0 ** (-5.0 - h))))
                def emit_scores(j, h=h):
                    jsl = slice(j * 128, (j + 1) * 128)
                    sb_ = j % 2
                    for dc in range(2):
                        blk.add("pe", lambda e, dc=dc, jsl=jsl: e.matmul(psS[:, 0:128], kf[:, dc, jsl], qf[:, dc, jsl],
                                                                       start=(dc == 0), stop=(dc == 1)),
                                reads=["k", "q"], writes=["S"])
                    blk.add("dve", lambda e, h=h, sb_=sb_: e.tensor_tensor(sT[:, sb_, :], psS[:, 0:128], cmask[:, h, :], ALU.mult),
                            reads=["S", "cmask"], writes=[("sT", sb_)])

                def emit_transposes(j):
                    jsl = slice(j * 128, (j + 1) * 128)
                    ob = j % 2
                    tr = trot.next()
                    for vc in range(4):
                        blk.add("pe", lambda e, vc=vc, tr=tr, ob=ob: e.transpose(psT[:, tr, vc, :], onb[:, ob, vc * 128:(vc + 1) * 128], ident[:]),
                                reads=[("onb", ob), "ident"], writes=[("T", tr)])
                    blk.add("act", lambda e, tr=tr, jsl=jsl: e.copy(ogs[:, :, jsl], psT[:, tr, 0:4, :]),
                            reads=[("T", tr)], writes=["ogs"])

                emit_scores(0)
                for j in range(NJ):
                    jsl = slice(j * 128, (j + 1) * 128)
                    cur, nxt = j % 2, (j + 1) % 2
                    if j + 1 < NJ:
                        emit_scores(j + 1)
                    for dc in range(2):
                        blk.add("pe", lambda e, j=j, dc=dc: e.matmul(psU[:, dc, :], ktm[:, j, dc * 128:(dc + 1) * 128], vtm[:, j, :],
                                                                   start=True, stop=True),
                                reads=["ktm", "vtm"], writes=[("U", dc)])
                        blk.add("dve", lambda e, dc=dc, sdec=sdec: e.scalar_tensor_tensor(
                            stf[:, dc, :], stf[:, dc, :], sdec, psU[:, dc, :], op0=ALU.mult, op1=ALU.add),
                            reads=[("U", dc), "stf"], writes=["stf"])
                    blk.add("act", lambda e, nxt=nxt: e.copy(stb[:, nxt], stf[:]), reads=["stf"], writes=[("stb", nxt)])
                    pump(10)
                    blk.add("pe", lambda e, j=j, cur=cur: e.matmul(psO[:], sT[:, cur, :], vtm[:, j, :], start=True, stop=False),
                            reads=[("sT", cur), "vtm"], writes=["O"])
                    for dc in range(2):
                        blk.add("pe", lambda e, dc=dc, jsl=jsl, cur=cur: e.matmul(psO[:], qf[:, dc, jsl], stb[:, cur, dc, :],
                                                                                start=False, stop=(dc == 1)),
                                reads=["q", ("stb", cur)], writes=["O"])
                    blk.add("act", lambda e, cur=cur: e.activation(junk[:], psO[:], AF.Square, accum_out=ssp[:, cur, 0:1]),
                            reads=["O"], writes=["junk", ("ssp", cur, 0)])
                    blk.add("dve", lambda e, h=h, cur=cur: e.tensor_scalar(ssp[:, cur, 1:2], ssp[:, cur, 0:1],
                                                                         ccol[:, 8 + 2 * H + h:9 + 2 * H + h], EPS,
                                                                         op0=ALU.mult, op1=ALU.add),
                            reads=[("ssp", cur, 0), "ccol"], writes=[("ssp", cur, 1)])
                    blk.add("act", lambda e, cur=cur: e.activation(ssp[:, cur, 2:3], ssp[:, cur, 1:2], AF.Sqrt),
                            reads=[("ssp", cur, 1)], writes=[("ssp", cur, 2)])
                    blk.add("dve", lambda e, cur=cur: e.reciprocal(ssp[:, cur, 2:3], ssp[:, cur, 2:3]),
                            reads=[("ssp", cur, 2)], writes=[("ssp", cur, 2)])
                    blk.add("dve", lambda e, h=h, cur=cur: e.tensor_tensor(ssp[:, cur, 3:4], ssp[:, cur, 2:3], ccol[:, 8 + h:9 + h], ALU.mult),
                            reads=[("ssp", cur, 2), "ccol"], writes=[("ssp", cur, 3)])
                    blk.add("dve", lambda e, cur=cur: e.tensor_scalar(onb[:, cur, :], psO[:], ssp[:, cur, 3:4], None, op0=ALU.mult),
                            reads=["O", ("ssp", cur, 3)], writes=[("onb", cur)])
                    pump(11)
                    if j >= 1:
                        emit_transposes(j - 1)
                    pump(11)
                emit_transposes(NJ - 1)
                pump(10 ** 9)
                blk.add("dve", lambda e: e.tensor_tensor(ogs[:], ogs[:], sg[:], ALU.mult), reads=["ogs", "sg"], writes=["ogs"])
                blk.dma("sp", self.rst[h], stf[:], reads=["stf"])
                blk.dma("sp", self.og[h * 4:(h + 1) * 4].rearrange("a p t -> p a t"), ogs[:], reads=["ogs"])
            blk.emit()

    @_phase
    def proj_resid(self, name, x_dram, kp, KCn, w_dram, h_src, h_dst, tok_blocks):
        nc, c = self.nc, self.c
        blk = Blk(nc, name)
        with ExitStack() as es:
            TB = tok_blocks[0][1]
            x_fm = es.enter_context(nc.sbuf_tensor(f"{name}_x", [128, KCn, TB], BF16))
            ps = es.enter_context(nc.psum_tensor(f"{name}_ps", [128, 4, 512], F32))
            banks = Rot(4)
            ep = self.resid_epilogue(blk, es, ps, h_src, h_dst)
            nsl = 2 if KCn * 128 * 2 > 16384 else 3
            wsl = es.enter_context(nc.sbuf_tensor(f"{name}_wsl", [128, nsl, KCn, 128], BF16))
            wrot = Rot(nsl)
            G = 8 if KCn % 8 == 0 else 1
            per = KCn // G
            for (B0, BW) in tok_blocks:
                for g in range(G):
                    blk.dma("sp", x_fm[0:kp, g * per:(g + 1) * per, 0:BW],
                            x_dram[g * per:(g + 1) * per, :, B0:B0 + BW].rearrange("a p t -> p a t"),
                            writes=[("x", g)])
                for n in range(c.KC):
                    s = wrot.next()
                    blk.dma("pool", wsl[0:kp, s, :, :], w_dram[n], writes=[("w", s)], max_dma_last_dim=8192)
                    for t0 in range(0, BW, c.TN):
                        tw = min(c.TN, BW - t0)
                        b = banks.next()
                        for kc in range(KCn):
                            blk.add("pe", lambda e, s=s, kc=kc, b=b, t0=t0, tw=tw: e.matmul(
                                ps[:, b, 0:tw], wsl[0:kp, s, kc, :], x_fm[0:kp, kc, t0:t0 + tw],
                                start=(kc == 0), stop=(kc == KCn - 1)),
                                reads=[("w", s), ("x", kc // per)], writes=[("ps", b)])
                        ep(n, B0 + t0, tw, b)
            blk.emit()

    @_phase
    def mlp_up(self, name, y_fm, layer):
        nc, c = self.nc, self.c
        TN, SEG = c.TN, c.SEG
        blk = Blk(nc, name)
        with ExitStack() as es:
            ps = es.enter_context(nc.psum_tensor(f"{name}_ps", [128, 4, 512], F32))
            rl = es.enter_context(nc.sbuf_tensor(f"{name}_rl", [128, 2, TN], F32))
            zs = es.enter_context(nc.sbuf_tensor(f"{name}_zs", [128, 2, SEG], BF16))
            rrot = Rot(2)

            def ep(n, t0, tw, b):
                s = rrot.next()
                zb = n % 2
                blk.add("act", lambda e: e.activation(rl[:, s, 0:tw], ps[:, b, 0:tw], AF.Relu),
                        reads=[("ps", b)], writes=[("rl", s)])
                blk.add("dve", lambda e: e.tensor_tensor(zs[:, zb, t0:t0 + tw], rl[:, s, 0:tw], rl[:, s, 0:tw], ALU.mult),
                        reads=[("rl", s)], writes=[("zs", zb)])
                if t0 + tw == SEG:
                    blk.dma("sp", self.z[n], zs[:, zb, :], reads=[("zs", zb)])

            tn_list = [(t * TN, TN) for t in range(c.NT)]
            self.gemm(blk, es, [self.w_up[layer, n] for n in range(c.FC)], 128, c.KC, 128, y_fm, "y", 4,
                      Rot(4), ps, ep, tn_list)
            blk.emit()

    @_phase
    def lru_a(self, name, y_fm, st, first):
        nc, c = self.nc, self.c
        KC, TN, SEG, LC = c.KC, c.TN, c.SEG, c.LC
        tail, hlast, cst = st
        blk = Blk(nc, name)
        with ExitStack() as es:
            sb = lambda nm, shape, dt=F32: es.enter_context(nc.sbuf_tensor(f"{name}_{nm}", shape, dt))
            lp = sb("lp", [112, LC, 8])
            blk.dma("sp", lp[:], self.lru_p, writes=["lp"])
            if first:
                blk.add("pool", lambda e: e.memset(tail[:], 0.0), writes=["tail"])
                blk.add("pool", lambda e: e.memset(hlast[:], 0.0), writes=["hlast"])
                blk.add("act", lambda e: e.activation(cst[:], lp[:, :, 7], AF.Exp, scale=-1.0), reads=["lp"], writes=["cst"])
                blk.add("dve", lambda e: e.tensor_scalar(cst[:], cst[:], 1.0, None, op0=ALU.add), reads=["cst"], writes=["cst"])
                blk.add("act", lambda e: e.activation(cst[:], cst[:], AF.Ln), reads=["cst"], writes=["cst"])
                blk.add("act", lambda e: e.mul(cst[:], cst[:], -LRU_C), reads=["cst"], writes=["cst"])
            NSL = 4
            wsl = sb("wsl", [128, NSL, KC, 112], BF16)
            wrot = Rot(NSL)
            gw = sb("gw", [112, 2, 2, 3, 336], BF16)
            ps = es.enter_context(nc.psum_tensor(f"{name}_ps", [128, 4, 512], F32))
            psG = es.enter_context(nc.psum_tensor(f"{name}_psG", [128, 4, 512], F32))
            banks = Rot(4)
            gbanks = Rot(4)
            xb = sb("xb", [112, 3, SEG + 3])
            xc = sb("xc", [112, 3, SEG])
            xcb = sb("xcb", [112, 3, SEG], BF16)
            rg = sb("rg", [112, 3, SEG])
            ig = sb("ig", [112, 3, SEG])
            av = sb("av", [112, SEG])
            t1 = sb("t1", [112, SEG])
            t2 = sb("t2", [112, SEG])
            hs = sb("hs", [112, SEG])
            gx = sb("gx", [112, 3, SEG])
            gt = sb("gt", [112, SEG])
            ygs = sb("ygs", [112, 2, SEG], BF16)
            yrot = Rot(2)

            def load_w(n):
                s = wrot.next()
                blk.dma("pool", wsl[:, s, :, :], self.w_li[n], writes=[("w", s)], max_dma_last_dim=8192)
                return s

            def mm_chunk(s, t0, tw):
                b = banks.next()
                for kc in range(KC):
                    blk.add("pe", lambda e, kc=kc: e.matmul(ps[0:112, b, 0:tw], wsl[:, s, kc, :], y_fm[:, kc, t0:t0 + tw],
                                                           start=(kc == 0), stop=(kc == KC - 1)),
                            reads=[("w", s)], writes=[("ps", b)])
                return b

            for n in range(c.LB):
                gs = n % 2
                blk.dma("pool", gw[:, gs], self.w_gt[n], writes=[("gw", gs)], max_dma_last_dim=8192)
                for i in range(3):
                    ch = 3 * n + i
                    s = load_w(ch)
                    blk.add("dve", lambda e, i=i, ch=ch: e.tensor_copy(xb[:, i, 0:3], tail[:, ch, :]),
                            reads=["tail"], writes=[("xb", i)])
                    for tt in range(c.NT):
                        t0 = tt * TN
                        b = mm_chunk(s, t0, TN)
                        blk.add("act", lambda e, i=i, t0=t0, b=b: e.copy(xb[:, i, 3 + t0:3 + t0 + TN], ps[0:112, b, 0:TN]),
                                reads=[("ps", b)], writes=[("xb", i)])
                    blk.add("dve", lambda e, i=i, ch=ch: e.tensor_copy(tail[:, ch, :], xb[:, i, SEG:SEG + 3]),
                            reads=[("xb", i)], writes=["tail"])
                    blk.add("act", lambda e, i=i, ch=ch: e.activation(xc[:, i, :], xb[:, i, 3:3 + SEG], AF.Identity,
                                                                      bias=lp[:, ch, 4:5], scale=lp[:, ch, 3:4]),
                            reads=[("xb", i), "lp"], writes=[("xc", i)])
                    for jj in range(3):
                        blk.add("dve", lambda e, i=i, ch=ch, jj=jj: e.scalar_tensor_tensor(
                            xc[:, i, :], xb[:, i, jj:jj + SEG], lp[:, ch, jj:jj + 1], xc[:, i, :], op0=ALU.mult, op1=ALU.add),
                            reads=[("xb", i), "lp", ("xc", i)], writes=[("xc", i)])
                    blk.add("act", lambda e, i=i: e.copy(xcb[:, i, :], xc[:, i, :]), reads=[("xc", i)], writes=[("xcb", i)])
                for i in range(3):
                    ch = 3 * n + i
                    s = load_w(LC + ch)
                    for tt in range(c.NT):
                        t0 = tt * TN
                        b = mm_chunk(s, t0, TN)
                        blk.add("act", lambda e, i=i, t0=t0, b=b: e.copy(gx[:, i, t0:t0 + TN], ps[0:112, b, 0:TN]),
                                reads=[("ps", b)], writes=[("gx", i)])
                for gi, (dst, bcol) in enumerate(((rg, 5), (ig, 6))):
                    for jd in range(3):
                        ch = 3 * n + jd
                        for tt in range(c.NT):
                            t0 = tt * TN
                            b = gbanks.next()
                            for i in range(3):
                                blk.add("pe", lambda e, gi=gi, jd=jd, i=i, b=b, t0=t0, gs=gs: e.matmul(
                                    psG[0:112, b, 0:TN], gw[:, gs, gi, i, jd * 112:(jd + 1) * 112], xcb[:, i, t0:t0 + TN],
                                    start=(i == 0), stop=(i == 2)),
                                    reads=[("gw", gs), ("xcb", i)], writes=[("psG", b)])
                            blk.add("act", lambda e, dst=dst, jd=jd, ch=ch, bcol=bcol, b=b, t0=t0: e.activation(
                                dst[:, jd, t0:t0 + TN], psG[0:112, b, 0:TN], AF.Sigmoid, bias=lp[:, ch, bcol:bcol + 1]),
                                reads=[("psG", b), "lp"], writes=[("gate", gi, jd)])
                for i in range(3):
                    ch = 3 * n + i
                    blk.add("act", lambda e, i=i, ch=ch: e.activation(av[:], rg[:, i, :], AF.Exp, scale=cst[:, ch:ch + 1]),
                            reads=[("gate", 0, i), "cst"], writes=["av"])
                    blk.add("dve", lambda e: e.tensor_tensor(t1[:], av[:], av[:], ALU.mult), reads=["av"], writes=["t1"])
                    blk.add("dve", lambda e: e.tensor_scalar(t1[:], t1[:], -1.0, 1.0, op0=ALU.mult, op1=ALU.add),
                            reads=["t1"], writes=["t1"])
                    blk.add("dve", lambda e: e.tensor_scalar(t1[:], t1[:], 1e-30, None, op0=ALU.max),
                            reads=["t1"], writes=["t1"])
                    blk.add("act", lambda e: e.activation(t1[:], t1[:], AF.Sqrt), reads=["t1"], writes=["t1"])
                    blk.add("dve", lambda e, i=i: e.tensor_tensor(t2[:], ig[:, i, :], xc[:, i, :], ALU.mult),
                            reads=[("gate", 1, i), ("xc", i)], writes=["t2"])
                    blk.add("dve", lambda e: e.tensor_tensor(t2[:], t2[:], t1[:], ALU.mult), reads=["t1", "t2"], writes=["t2"])
                    blk.add("dve", lambda e, ch=ch: e.tensor_tensor_scan(hs[:], av[:], t2[:], hlast[:, ch:ch + 1],
                                                                      op0=ALU.mult, op1=ALU.add),
                            reads=["av", "t2", "hlast"], writes=["hs"])
                    blk.add("dve", lambda e, ch=ch: e.tensor_copy(hlast[:, ch:ch + 1], hs[:, SEG - 1:SEG]),
                            reads=["hs"], writes=["hlast"])
                    blk.add("dve", lambda e, i=i: e.tensor_tensor(gt[:], gx[:, i, :], gx[:, i, :], ALU.mult), reads=[("gx", i)], writes=["gt"])
                    blk.add("dve", lambda e: e.tensor_scalar(gt[:], gt[:], 0.044715, 1.0, op0=ALU.mult, op1=ALU.add),
                            reads=["gt"], writes=["gt"])
                    blk.add("dve", lambda e, i=i: e.tensor_tensor(gt[:], gt[:], gx[:, i, :], ALU.mult), reads=["gt", ("gx", i)], writes=["gt"])
                    blk.add("act", lambda e: e.activation(gt[:], gt[:], AF.Sigmoid, scale=1.5957691216057308),
                            reads=["gt"], writes=["gt"])
                    blk.add("dve", lambda e, i=i: e.tensor_tensor(gt[:], gt[:], gx[:, i, :], ALU.mult), reads=["gt", ("gx", i)], writes=["gt"])
                    ys = yrot.next()
                    blk.add("dve", lambda e, ys=ys: e.tensor_tensor(ygs[:, ys, :], hs[:], gt[:], ALU.mult),
                            reads=["hs", "gt"], writes=[("ygs", ys)])
                    blk.dma("sp", self.yg[ch], ygs[:, ys, :], reads=[("ygs", ys)])
            blk.emit()

    def build(self, max_phase=10**9):
        nc, c = self.nc, self.c
        self._ph = 0
        self._maxph = max_phase
        with ExitStack() as top:
            tail = top.enter_context(nc.sbuf_tensor("lru_tail", [112, c.LC, 3], F32))
            hlast = top.enter_context(nc.sbuf_tensor("lru_hlast", [112, c.LC], F32))
            cst = top.enter_context(nc.sbuf_tensor("lru_cst", [112, c.LC], F32))
            full = [(0, c.SEG)]
            zblocks = [(t * c.TN, c.TN) for t in range(c.NT)]

            def with_y(fn):
                with nc.sbuf_tensor(f"y_fm_{nc.next_id()}", [128, c.KC, c.SEG], BF16) as y_fm:
                    fn(y_fm)

            for seg in range(c.NSEG):
                first = (seg % c.SPS == 0)
                p = f"s{seg}"

                def l0(y_fm):
                    self.norm(p + "n0", self.xT[seg], 0, y_fm=y_fm)
                    self.ret_a(p + "ra", y_fm, seg, first)
                with_y(l0)
                self.proj_resid(p + "ro", self.og, 128, c.H * 4, self.w_ro, self.xT[seg], self.hA, full)

                def m0(y_fm):
                    self.norm(p + "n1", self.hA, 2, y_fm=y_fm)
                    self.mlp_up(p + "u0", y_fm, 0)
                with_y(m0)
                self.proj_resid(p + "d0", self.z, 128, c.FC, self.w_dn[0], self.hA, self.hB, zblocks)

                def l1(y_fm):
                    self.norm(p + "n2", self.hB, 1, y_fm=y_fm)
                    self.lru_a(p + "la", y_fm, (tail, hlast, cst), first)
                with_y(l1)
                self.proj_resid(p + "lo", self.yg, 112, c.LC, self.w_lo, self.hB, self.hC, full)

                def m1(y_fm):
                    self.norm(p + "n3", self.hC, 3, y_fm=y_fm)
                    self.mlp_up(p + "u1", y_fm, 1)
                with_y(m1)
                self.proj_resid(p + "d1", self.z, 128, c.FC, self.w_dn[1], self.hC, self.hD, zblocks)
                self.norm(p + "nf", self.hD, 4, out_dram=self.outT[seg])
        return nc


def _tile_w(W, kp, M, colperm=None):
    K, N = W.shape
    if colperm is not None:
        W = W[:, colperm]
    return np.ascontiguousarray(W.reshape(K // kp, kp, N // M, M).transpose(2, 1, 0, 3))


def _consts(cfg):
    H = cfg.H
    k = np.arange(128)[:, None].astype(np.float64)
    q = np.arange(128)[None, :].astype(np.float64)
    cmask = np.zeros((128, H, 128), np.float32)
    ccol = np.zeros((128, 8 + 3 * H), np.float32)
    ccol[:, 0] = (np.float32(ROPE_THETA) ** (-(np.arange(128, dtype=np.float32) / np.float32(128)))).astype(np.float32)
    ccol[:, 1] = np.float32(np.pi)
    t = np.arange(128, dtype=np.float64)
    for h in range(H):
        lg = np.log1p(-2.0 ** (-5.0 - h))
        m = np.where((k // 64) <= (q // 64), np.exp(lg * (np.abs(q - k) - (q + 1.0))), 0.0)
        cmask[:, h, :] = m.astype(np.float32)
        qdec = np.exp(lg * (t + 1.0))
        ccol[:, 8 + h] = qdec
        ccol[:, 8 + H + h] = np.exp(lg * (127.0 - t))
        ccol[:, 8 + 2 * H + h] = qdec * qdec / cfg.DV
    return cmask, ccol, np.eye(128, dtype=np.float32)


def prepare_inputs(cfg, x, positions, norm_mix_g, norm_mlp_g, final_norm_g, ret_w_in, ret_w_out,
                   lru_w_in, lru_conv_w, lru_conv_b, lru_w_rgate, lru_b_rgate, lru_w_igate,
                   lru_b_igate, lru_lambda, lru_w_out, mlp_w_up, mlp_w_down):
    c = cfg
    f = lambda a: np.asarray(a, dtype=np.float32)
    D, H, LC, LW, KC = c.D, c.H, c.LC, c.LW, c.KC
    shared = {}
    gl = np.stack([f(norm_mix_g)[0], f(norm_mix_g)[1], f(norm_mlp_g)[0], f(norm_mlp_g)[1], f(final_norm_g)], 0)
    shared["gains"] = np.ascontiguousarray(gl.reshape(5, KC, 128).transpose(2, 0, 1))
    perm = []
    for h in range(H):
        perm += list(range(h * 256, (h + 1) * 256))
        perm += list(range(D + h * 256, D + (h + 1) * 256))
        perm += list(range(2 * D + h * 512, 2 * D + (h + 1) * 512))
        perm += list(range(2 * D + c.HV + h * 512, 2 * D + c.HV + (h + 1) * 512))
    shared["w_in"] = _tile_w(f(ret_w_in)[0], 128, 128, np.asarray(perm))
    shared["w_ro"] = _tile_w(f(ret_w_out)[0], 128, 128)
    shared["w_li"] = _tile_w(f(lru_w_in)[0], 128, 112)
    shared["w_lo"] = _tile_w(f(lru_w_out)[0], 112, 128)
    shared["w_up"] = np.stack([_tile_w(f(mlp_w_up)[l], 128, 128) for l in range(2)], 0)
    shared["w_dn"] = np.stack([_tile_w(f(mlp_w_down)[l], 128, 128) for l in range(2)], 0)
    wr, wi = f(lru_w_rgate)[0], f(lru_w_igate)[0]
    wg = np.stack([wr, wi], 1)
    shared["w_gt"] = np.ascontiguousarray(wg.reshape(c.LB, 2, 3, 112, 336).transpose(0, 3, 1, 2, 4))
    lp = np.zeros((112, LC, 8), np.float32)
    cw = f(lru_conv_w)[0]
    for j in range(4):
        lp[:, :, j] = cw[j].reshape(LC, 112).T
    lp[:, :, 4] = f(lru_conv_b)[0].reshape(LC, 112).T
    lp[:, :, 5] = f(lru_b_rgate)[0].reshape(LC, 112).T
    lp[:, :, 6] = f(lru_b_igate)[0].reshape(LC, 112).T
    lp[:, :, 7] = f(lru_lambda)[0].reshape(LC, 112).T
    shared["lru_p"] = lp
    shared["cmask"], shared["ccol"], shared["ident"] = _consts(c)
    xs = f(x)
    pos = np.asarray(positions).astype(np.int32)
    in_maps = []
    for core in range(c.NCORES):
        xb = xs[core * c.SPC:(core + 1) * c.SPC]
        xT = np.ascontiguousarray(xb.reshape(c.SPC * c.SPS, c.SEG, D).transpose(0, 2, 1))
        pp = np.ascontiguousarray(pos[core * c.SPC:(core + 1) * c.SPC].reshape(c.NSEG, c.SEG))
        m = dict(shared)
        m["xT"] = xT
        m["pos"] = pp
        in_maps.append(m)
    return in_maps


def run(cfg, inputs, debug=False, trace=False, max_phase=10**9):
    b = Builder(cfg, debug=debug)
    nc = b.build(max_phase)
    in_maps = prepare_inputs(cfg, **inputs)
    res = run_bass_kernel_spmd(nc, in_maps, core_ids=list(range(cfg.NCORES)), **({"trace": True} if trace else {}))
    outs = []
    for core in range(cfg.NCORES):
        oT = res.results[core]["outT"]
        outs.append(oT.transpose(0, 2, 1).reshape(cfg.SPC, cfg.SEQ, cfg.D))
    out = np.ascontiguousarray(np.concatenate(outs, 0)).astype(np.float32)
    return out, res


def kernel(**inputs):
    cfg = Cfg()
    out, _ = run(cfg, inputs)
    return out
```
